# Optimizing a Trainium2 kernel written in Bass

```python
import jax, jax.numpy as jnp
from jax import lax
import numpy as np

D_MODEL = 1024
BATCH = 8
SEQ = 2048
DEPTH = 1

PLE_DIM = 256
EPS = 1e-6
A_HEADS = 4
A_DQK = 128
A_DV = 256
A_QK_WIDTH = A_HEADS * A_DQK
A_WIDTH = A_HEADS * A_DV
A_CHUNK = 64
CONV_K = 4
B_HEADS = 16
B_DH = 64
B_WIDTH = B_HEADS * B_DH
Q_BLOCK = 128

IN_SPLITS = (A_QK_WIDTH, A_QK_WIDTH, A_WIDTH, A_HEADS, A_HEADS, A_WIDTH, A_WIDTH,
             B_WIDTH, B_WIDTH, B_WIDTH, B_HEADS, B_WIDTH, D_MODEL, D_MODEL)
N_IN = sum(IN_SPLITS)

kernel_name = "hybrid_mlstm_fox_gated_parallel"


def rms_norm(x, g):
    xf = x.astype(jnp.float32)
    y = xf * lax.rsqrt(jnp.mean(xf * xf, axis=-1, keepdims=True) + EPS)
    return (y * g.astype(jnp.float32)).astype(x.dtype)


def split_proj(proj):
    idx = np.cumsum(np.array(IN_SPLITS))[:-1].tolist()
    return jnp.split(proj, idx, axis=-1)


def heads(t, n_heads):
    b, s, c = t.shape
    return t.reshape(b, s, n_heads, c // n_heads).transpose(0, 2, 1, 3)


def merge_heads(t):
    b, h, s, d = t.shape
    return t.transpose(0, 2, 1, 3).reshape(b, s, h * d)


def causal_conv(x, w, b):
    s = x.shape[1]
    xp = jnp.pad(x, ((0, 0), (CONV_K - 1, 0), (0, 0)))
    y = xp[:, 0:s, :] * w[0]
    for kk in range(1, CONV_K):
        y = y + xp[:, kk:kk + s, :] * w[kk]
    return y + b


def mlstm_chunkwise(q, k, v, log_i, log_f):
    bsz, nh, s, dk = q.shape
    dv = v.shape[-1]
    nc = s // A_CHUNK

    def to_chunks(t):
        t = t.reshape((bsz, nh, nc, A_CHUNK) + t.shape[3:])
        return jnp.moveaxis(t, 2, 0)

    xs = (to_chunks(q), to_chunks(k), to_chunks(v), to_chunks(log_i), to_chunks(log_f))
    causal = jnp.tril(jnp.ones((A_CHUNK, A_CHUNK), dtype=bool))

    def step(carry, chunk):
        c_st, n_st, m_st = carry
        qj, kj, vj, ij, fj = chunk
        b = jnp.cumsum(fj, axis=-1)
        g = b[..., -1]
        dmat = b[..., :, None] - b[..., None, :] + ij[..., None, :]
        dmat = jnp.where(causal, dmat, -jnp.inf)
        inter = b + m_st[..., None]
        m_row = jnp.maximum(inter, jnp.max(dmat, axis=-1))
        w_intra = jnp.exp(dmat - m_row[..., None])
        w_inter = jnp.exp(inter - m_row)
        scores = jnp.einsum("bhld,bhsd->bhls", qj, kj) * w_intra
        num = (jnp.einsum("bhls,bhsv->bhlv", scores, vj)
               + w_inter[..., None] * jnp.einsum("bhld,bhdv->bhlv", qj, c_st))
        den = jnp.sum(scores, axis=-1) + w_inter * jnp.einsum("bhld,bhd->bhl", qj, n_st)
        h = num / jnp.maximum(jnp.abs(den), jnp.exp(-m_row))[..., None]
        to_end = g[..., None] - b + ij
        m_new = jnp.maximum(g + m_st, jnp.max(to_end, axis=-1))
        w_k = jnp.exp(to_end - m_new[..., None])
        decay = jnp.exp(g + m_st - m_new)
        c_new = decay[..., None, None] * c_st + jnp.einsum("bhs,bhsd,bhsv->bhdv", w_k, kj, vj)
        n_new = decay[..., None] * n_st + jnp.einsum("bhs,bhsd->bhd", w_k, kj)
        return (c_new, n_new, m_new), h

    init = (jnp.zeros((bsz, nh, dk, dv), jnp.float32),
            jnp.zeros((bsz, nh, dk), jnp.float32),
            jnp.zeros((bsz, nh), jnp.float32))
    _, hs = lax.scan(step, init, xs)
    return jnp.moveaxis(hs, 0, 2).reshape(bsz, nh, s, dv)


def forgetting_attention(q, k, v, log_f):
    s = q.shape[2]
    cum = jnp.cumsum(log_f, axis=-1)
    scale = B_DH ** -0.5
    outs = []
    for blk in range(s // Q_BLOCK):
        q0, q1 = blk * Q_BLOCK, (blk + 1) * Q_BLOCK
        qb = q[:, :, q0:q1]
        kb = k[:, :, :q1]
        vb = v[:, :, :q1]
        logits = (jnp.einsum("bhqd,bhkd->bhqk", qb, kb) * scale
                  + cum[:, :, q0:q1, None] - cum[:, :, None, :q1])
        mask = (q0 + jnp.arange(Q_BLOCK))[:, None] >= jnp.arange(q1)[None, :]
        logits = jnp.where(mask, logits, -jnp.inf)
        probs = jax.nn.softmax(logits, axis=-1)
        outs.append(jnp.einsum("bhqk,bhkd->bhqd", probs, vb))
    return jnp.concatenate(outs, axis=2)


def setup_inputs(seed: int = 0) -> dict:
    key = jax.random.key(seed)
    ks = jax.random.split(key, 17)
    f32 = jnp.float32

    def nrm(k, shape, scale):
        return jax.random.normal(k, shape, f32) * scale

    return {
        "x": nrm(ks[0], (BATCH, SEQ, D_MODEL), 1.0),
        "p": nrm(ks[1], (DEPTH, BATCH, SEQ, PLE_DIM), 1.0),
        "attn_norm_g": 1.0 + nrm(ks[2], (DEPTH, D_MODEL), 0.02),
        "w_in": nrm(ks[3], (DEPTH, D_MODEL, N_IN), D_MODEL ** -0.5),
        "conv_w": nrm(ks[4], (DEPTH, CONV_K, 2 * A_QK_WIDTH), 0.5),
        "conv_b": nrm(ks[5], (DEPTH, 2 * A_QK_WIDTH), 0.02),
        "a_bias_i": nrm(ks[6], (DEPTH, A_HEADS), 0.1),
        "a_bias_f": jnp.linspace(3.0, 6.0, A_HEADS, dtype=f32)[None, :] + nrm(ks[7], (DEPTH, A_HEADS), 0.1),
        "a_head_norm_g": 1.0 + nrm(ks[8], (DEPTH, A_WIDTH), 0.02),
        "b_bias_f": jnp.linspace(1.0, 4.0, B_HEADS, dtype=f32)[None, :] + nrm(ks[9], (DEPTH, B_HEADS), 0.1),
        "w_branch_a": nrm(ks[10], (DEPTH, A_WIDTH, D_MODEL), A_WIDTH ** -0.5),
        "w_branch_b": nrm(ks[11], (DEPTH, B_WIDTH, D_MODEL), B_WIDTH ** -0.5),
        "w_out": nrm(ks[12], (DEPTH, D_MODEL, D_MODEL), D_MODEL ** -0.5),
        "ple_norm_g": 1.0 + nrm(ks[13], (DEPTH, D_MODEL), 0.02),
        "w_ple_gate": nrm(ks[14], (DEPTH, D_MODEL, D_MODEL), D_MODEL ** -0.5),
        "w_ple_proj": nrm(ks[15], (DEPTH, PLE_DIM, D_MODEL), PLE_DIM ** -0.5),
        "final_norm_g": 1.0 + nrm(ks[16], (D_MODEL,), 0.02),
    }


def reference(x, p, attn_norm_g, w_in, conv_w, conv_b, a_bias_i, a_bias_f, a_head_norm_g,
              b_bias_f, w_branch_a, w_branch_b, w_out, ple_norm_g, w_ple_gate, w_ple_proj,
              final_norm_g):
    f32 = jnp.float32
    bsz, s, _ = x.shape
    for i in range(DEPTH):
        h = rms_norm(x, attn_norm_g[i])
        proj = h @ w_in[i]
        (a_q, a_k, a_v, a_i, a_f, a_o, a_z,
         b_q, b_k, b_v, b_f, b_z, g_a, g_b) = split_proj(proj)

        qk = jax.nn.silu(causal_conv(jnp.concatenate([a_q, a_k], axis=-1), conv_w[i], conv_b[i]))
        a_q, a_k = jnp.split(qk, 2, axis=-1)
        q_a = heads(a_q, A_HEADS).astype(f32)
        k_a = heads(a_k, A_HEADS).astype(f32) * (A_DQK ** -0.5)
        v_a = heads(a_v, A_HEADS).astype(f32)
        log_i = (a_i.astype(f32) + a_bias_i[i].astype(f32)).transpose(0, 2, 1)
        log_fa = jax.nn.log_sigmoid(a_f.astype(f32) + a_bias_f[i].astype(f32)).transpose(0, 2, 1)
        ha = mlstm_chunkwise(q_a, k_a, v_a, log_i, log_fa)
        ha = ha * lax.rsqrt(jnp.mean(ha * ha, axis=-1, keepdims=True) + EPS)
        ha = merge_heads(ha) * a_head_norm_g[i].astype(f32)
        ha = (jax.nn.sigmoid(a_o.astype(f32)) * ha).astype(x.dtype) * jax.nn.silu(a_z)
        y_a = ha @ w_branch_a[i]

        q_b = heads(b_q, B_HEADS).astype(f32)
        k_b = heads(b_k, B_HEADS).astype(f32)
        v_b = heads(b_v, B_HEADS).astype(f32)
        log_fb = jax.nn.log_sigmoid(b_f.astype(f32) + b_bias_f[i].astype(f32)).transpose(0, 2, 1)
        hb = forgetting_attention(q_b, k_b, v_b, log_fb)
        hb = merge_heads(hb).astype(x.dtype) * jax.nn.silu(b_z)
        y_b = hb @ w_branch_b[i]

        merged = jax.nn.sigmoid(g_a) * y_a + jax.nn.sigmoid(g_b) * y_b
        x = x + merged @ w_out[i]

        gate = jax.nn.sigmoid(rms_norm(x, ple_norm_g[i]) @ w_ple_gate[i])
        x = x + gate * (p[i] @ w_ple_proj[i])
    return rms_norm(x, final_norm_g)
```

```python
import numpy as np
from contextlib import ExitStack
import concourse.bass as bass
import concourse.mybir as mybir
from concourse.bass_utils import run_bass_kernel_spmd
from concourse.alu_op_type import AluOpType as ALU

F32 = mybir.dt.float32
BF16 = mybir.dt.bfloat16
AF = mybir.ActivationFunctionType

S_LEN = 2048
D = 1024
NT = 16
EPS = 1e-6
O_AQ, O_AK, O_AV, O_AI, O_AF, O_AO, O_AZ = 0, 512, 1024, 2048, 2052, 2056, 3080
O_BQ, O_BK, O_BV, O_BF, O_BZ, O_GA, O_GB = 4104, 5128, 6152, 7176, 7192, 8216, 9240
N_IN = 10264


ATTACH = True
PRUNE = True
MAXATT = 1
STOP_AFTER = None


class Buf:
    __slots__ = ("name", "w", "r")

    def __init__(self, name):
        self.name = name
        self.w = None
        self.r = {}


class Eng:
    def __init__(self, name):
        self.name = name
        self.q = []
        self.count = 0
        self.sem = None
        self.waited = {}


class Sched:
    def __init__(self):
        self.engs = {n: Eng(n) for n in ["pe", "act", "dve", "pool", "sp"]}
        self.dma_sems = {}
        self.ring = []
        self.ring_i = 0
        self.last_dma = {}
        self.muted = False

    def _wait(self, E, r, w, extra=()):
        deps = []
        for b in r:
            if b.w is not None:
                deps.append(b.w)
        for b in w:
            if b.w is not None:
                deps.append(b.w)
            deps.extend(b.r.values())
        deps.extend(extra)
        for (sem, val, key) in deps:
            if E.waited.get(key, 0) < val:
                E.waited[key] = val
                E.q.append(("wait", sem, val))

    def _commit(self, ticket, r, w):
        for b in r:
            old = b.r.get(ticket[2])
            if old is None or old[1] < ticket[1]:
                b.r[ticket[2]] = ticket
        for b in w:
            b.w = ticket
            b.r = {}

    def op(self, eng, fn, r=(), w=(), extra=()):
        if self.muted:
            return None
        E = self.engs[eng]
        self._wait(E, r, w, extra)
        E.count += 1
        E.q.append(("op", fn, E.sem, 1))
        t = (E.sem, E.count, eng)
        self._commit(t, r, w)
        return t

    def group(self, eng, fns, r=(), w=()):
        if self.muted:
            return None
        E = self.engs[eng]
        self._wait(E, r, w)
        for fn in fns[:-1]:
            E.q.append(("op", fn, None, 0))
        E.count += 1
        E.q.append(("op", fns[-1], E.sem, 1))
        t = (E.sem, E.count, eng)
        self._commit(t, r, w)
        return t

    def dma(self, eng, fn, stream, r=(), w=(), prefetch=False):
        if self.muted:
            return None
        E = self.engs[eng]
        self._wait(E, r, w)
        i = self.ring_i % len(self.ring)
        self.ring_i += 1
        s = self.ring[i]
        key = "dmaring_%d" % i
        if s[1] > 0 and E.waited.get(key, 0) < s[1]:
            E.waited[key] = s[1]
            E.q.append(("wait", s[0], s[1]))
        s[1] += 16
        s[2] = prefetch
        E.q.append(("op", fn, s[0], 16))
        t = (s[0], s[1], key)
        self._commit(t, r, w)
        self.last_dma[stream] = t
        return t

    def barrier(self):
        if self.muted:
            return
        tickets = []
        for n, E in self.engs.items():
            if E.count > 0:
                tickets.append((E.sem, E.count, n))
        for i, s in enumerate(self.ring):
            if s[1] > 0 and not s[2]:
                tickets.append((s[0], s[1], "dmaring_%d" % i))
        for n, E in self.engs.items():
            for (sem, val, key) in tickets:
                if key == n:
                    continue
                if E.waited.get(key, 0) < val:
                    E.waited[key] = val
                    E.q.append(("wait", sem, val))

    def finalize(self):
        needed = {n: set() for n in self.engs}
        sem2eng = {id(E.sem): n for n, E in self.engs.items()}
        for E in self.engs.values():
            for item in E.q:
                if item[0] == "wait" and id(item[1]) in sem2eng:
                    needed[sem2eng[id(item[1])]].add(item[2])
        self.remap = {}
        for n, E in self.engs.items():
            keep = sorted(needed[n])
            self.remap[n] = {old: i + 1 for i, old in enumerate(keep)}
        for n, E in self.engs.items():
            cnt = 0
            newq = []
            for item in E.q:
                if item[0] == "wait":
                    if id(item[1]) in sem2eng:
                        src = sem2eng[id(item[1])]
                        newq.append(("wait", item[1], self.remap[src][item[2]]))
                    else:
                        newq.append(item)
                else:
                    if item[2] is not None and id(item[2]) in sem2eng:
                        cnt += 1
                        if cnt in self.remap[n]:
                            newq.append(item)
                        else:
                            newq.append(("op", item[1], None, 0))
                    else:
                        newq.append(item)
            E.q = newq

    def replay(self, eng, e):
        pending = []
        for item in self.engs[eng].q:
            if item[0] == "wait":
                pending.append(item)
            else:
                if not ATTACH:
                    for w_ in pending:
                        e.wait_ge(w_[1], w_[2])
                    pending = []
                else:
                    for w_ in pending[:-MAXATT] if len(pending) > MAXATT else []:
                        e.wait_ge(w_[1], w_[2])
                    pending = pending[-MAXATT:]
                ins = item[1](e)
                for w_ in pending:
                    ins._wait_ge(w_[1], w_[2])
                pending = []
                if item[2] is not None:
                    ins.then_inc(item[2], item[3])


class Rot:
    def __init__(self, items):
        self.items = items
        self.i = 0

    def next(self):
        it = self.items[self.i % len(self.items)]
        self.i += 1
        return it


def build_nc(debug=False):
    nc = bass.Bass("TRN2", target_bir_lowering=False)

    def dram(name, shape, kind="ExternalInput"):
        return nc.dram_tensor(name, shape, F32, kind=kind).ap()

    x_d = dram("x", [S_LEN, D])
    p_d = dram("p", [S_LEN, 256])
    g_attn_d = dram("attn_norm_g", [1, D])
    win_d = dram("w_in", [D, N_IN])
    convw_d = dram("conv_w", [4, 1024])
    convb_d = dram("conv_b", [1, 1024])
    abi_d = dram("a_bias_i", [1, 4])
    abf_d = dram("a_bias_f", [1, 4])
    ghn_d = dram("a_head_norm_g", [1, 1024])
    bbf_d = dram("b_bias_f", [1, 16])
    wa_d = dram("w_branch_a", [D, D])
    wb_d = dram("w_branch_b", [D, D])
    wo_d = dram("w_out", [D, D])
    gple_d = dram("ple_norm_g", [1, D])
    wpg_d = dram("w_ple_gate", [D, D])
    wpp_d = dram("w_ple_proj", [256, D])
    gfin_d = dram("final_norm_g", [1, D])
    y_d = dram("y", [S_LEN, D], kind="ExternalOutput")
    dbg = {}
    dbg_tickets = []
    if debug:
        dbg["hT"] = nc.dram_tensor("dbg_hT", [128, 8, S_LEN], BF16, kind="ExternalOutput").ap()
        dbg["haGT"] = nc.dram_tensor("dbg_haGT", [128, 8, S_LEN], BF16, kind="ExternalOutput").ap()
        dbg["hbGT"] = nc.dram_tensor("dbg_hbGT", [128, 8, S_LEN], BF16, kind="ExternalOutput").ap()
        dbg["mT"] = nc.dram_tensor("dbg_mT", [128, 8, S_LEN], BF16, kind="ExternalOutput").ap()
        dbg["gates"] = nc.dram_tensor("dbg_gates", [128, 16, 24], F32, kind="ExternalOutput").ap()
        dbg["cum"] = nc.dram_tensor("dbg_cum", [128, 16, 16], F32, kind="ExternalOutput").ap()
        dbg["x1"] = nc.dram_tensor("dbg_x1", [128, 16, D], F32, kind="ExternalOutput").ap()

    win_v = win_d.rearrange("(kc p) n -> p kc n", p=128)

    S = Sched()
    with ExitStack() as top:
        for n in S.engs:
            S.engs[n].sem = top.enter_context(nc.semaphore("s_" + n))
        for i in range(20):
            S.ring.append([top.enter_context(nc.semaphore("dr%d" % i)), 0, False])

        def sbuf(st, name, shape, dt):
            return st.enter_context(nc.sbuf_tensor(name, shape, dt))

        P = [top.enter_context(nc.psum_tensor("P%d" % i, [128, 512], F32)) for i in range(6)]
        PB_ = [Buf("P%d" % i) for i in range(6)]
        T = [top.enter_context(nc.psum_tensor("T%d" % i, [128, 1024], BF16)) for i in range(2)]
        TB_ = [Buf("T%d" % i) for i in range(2)]
        P.append(T[0][:, :].bitcast(F32))
        PB_.append(TB_[0])
        P.append(T[1][:, :].bitcast(F32))
        PB_.append(TB_[1])

        def act(out, in_, func, r, w, **kw):
            return S.op("act", lambda e: e.activation(out=out, in_=in_, func=func, **kw), r, w)

        def tt(out, in0, in1, op, r, w, eng="dve"):
            return S.op(eng, lambda e: e.tensor_tensor(out=out, in0=in0, in1=in1, op=op), r, w)

        def ts(out, in0, s1, s2, op0, op1, r, w, eng="dve"):
            if op1 is None:
                return S.op(eng, lambda e: e.tensor_scalar(out=out, in0=in0, scalar1=s1, scalar2=None, op0=op0), r, w)
            return S.op(eng, lambda e: e.tensor_scalar(out=out, in0=in0, scalar1=s1, scalar2=s2, op0=op0, op1=op1), r, w)

        def stt(out, in0, scalar, in1, op0, op1, r, w):
            return S.op("dve", lambda e: e.scalar_tensor_tensor(out=out, in0=in0, scalar=scalar, in1=in1, op0=op0, op1=op1), r, w)

        def cp(out, in_, r, w, eng="dve"):
            return S.op(eng, lambda e: e.tensor_copy(out=out, in_=in_), r, w)

        def recip(out, in_, r, w):
            return S.op("dve", lambda e: e.reciprocal(out=out, in_=in_), r, w)

        def mm(out, lhsT, rhs, start, stop):
            return lambda e: e.matmul(out, lhsT=lhsT, rhs=rhs, start=start, stop=stop)

        def tr(out, in_, ident):
            return lambda e: e.transpose(out=out, in_=in_, identity=ident)

        def dma(eng, out, in_, stream, r, w, slow=False, prefetch=False):
            if slow:
                return S.dma(eng, lambda e: e.dma_start(out=out, in_=in_, allow_slow_non_contiguous=True), stream, r, w, prefetch)
            return S.dma(eng, lambda e: e.dma_start(out=out, in_=in_), stream, r, w, prefetch)

        def acc_group(ps_ap, pairs, r, w):
            n = len(pairs)
            fns = [mm(ps_ap, l, rr, i == 0, i == n - 1) for i, (l, rr) in enumerate(pairs)]
            return S.group("pe", fns, r, w)

        ident = sbuf(top, "ident", [128, 128], BF16)
        maskT = sbuf(top, "maskT", [128, 128], BF16)
        ones_bf = sbuf(top, "ones_bf", [128, 128], BF16)
        negmask = sbuf(top, "negmask", [128, 128], BF16)
        CONST = Buf("const")
        S.op("pool", lambda e: e.memset(ones_bf[:], 1.0), w=[CONST])
        S.op("pool", lambda e: e.memset(ident[:], 0.0), w=[CONST])
        S.op("pool", lambda e: e.affine_select(out=ident[:], in_=ident[:], pattern=[[-1, 128]], compare_op=ALU.not_equal,
                                               fill=1.0, base=0, channel_multiplier=1), w=[CONST])
        S.op("pool", lambda e: e.memset(negmask[:], 0.0), w=[CONST])
        S.op("pool", lambda e: e.affine_select(out=negmask[:], in_=negmask[:], pattern=[[1, 128]], compare_op=ALU.is_ge,
                                               fill=-240000.0, base=0, channel_multiplier=-1), w=[CONST])
        S.op("pool", lambda e: e.affine_select(out=maskT[:], in_=ones_bf[:], pattern=[[1, 128]], compare_op=ALU.is_ge,
                                               fill=0.0, base=0, channel_multiplier=-1), w=[CONST])

        m_eq = sbuf(top, "m_eq", [128, 16, 4], F32)
        m_ek = sbuf(top, "m_ek", [128, 16, 4], F32)
        m_eke = sbuf(top, "m_eke", [128, 16, 4], F32)
        m_eg = sbuf(top, "m_eg", [128, 16, 4], F32)
        f_cum = sbuf(top, "f_cum", [128, 16, 16], F32)
        f_carry = sbuf(top, "f_carry", [128, 17, 16], F32)
        SCAL = Buf("scal")

        mergedT = sbuf(top, "mergedT", [128, 8, S_LEN], BF16)
        MT = [Buf("mT%d" % i) for i in range(4)]

        with ExitStack() as sAE:
            hT = sbuf(sAE, "hT", [128, 8, S_LEN], BF16)
            HT = [Buf("hT%d" % t) for t in range(NT)]
            haGT = sbuf(sAE, "haGT", [128, 8, S_LEN], BF16)
            HA = Buf("haGT")

            wtA = sbuf(sAE, "wbuf16", [128, 8, 1024], BF16)
            wts = [wtA[:, :, :], mergedT[:, 4:8, :].rearrange("p a (b c) -> p (a b) c", c=1024)]
            WT = [Buf("wtC0"), Buf("wtC1")]
            WTQK = [Buf("wtCqk0"), Buf("wtCqk1")]

            def load_wC(h, after=(), part="all"):
                wt = wts[h % 2]
                WB_ = WT[h % 2]
                WQ_ = WTQK[h % 2]
                if part in ("all", "qk"):
                    dma("pool", wt[:, :, 0:128], win_v[:, :, O_AQ + 128 * h:O_AQ + 128 * (h + 1)], "w", list(after), [WQ_], prefetch=True)
                    dma("pool", wt[:, :, 128:256], win_v[:, :, O_AK + 128 * h:O_AK + 128 * (h + 1)], "w", list(after), [WQ_], prefetch=True)
                if part in ("all", "voz"):
                    dma("pool", wt[:, :, 256:512], win_v[:, :, O_AV + 256 * h:O_AV + 256 * (h + 1)], "w", list(after), [WB_], prefetch=True)
                    dma("pool", wt[:, :, 512:768], win_v[:, :, O_AO + 256 * h:O_AO + 256 * (h + 1)], "w", list(after), [WB_], prefetch=True)
                    dma("pool", wt[:, :, 768:1024], win_v[:, :, O_AZ + 256 * h:O_AZ + 256 * (h + 1)], "w", list(after), [WB_], prefetch=True)

            load_wC(0, part="qk")

            convw = sbuf(sAE, "convw", [128, 8, 4], F32)
            convb = sbuf(sAE, "convb", [128, 8], F32)
            CW = Buf("convw")
            wg = sbuf(sAE, "wg", [128, 8, 24], BF16)
            WG = Buf("wg")
            bias_g = sbuf(sAE, "bias_g", [128, 24], F32)
            BG = Buf("bias_g")
            dma("pool", wg[:, :, 0:8], win_v[:, :, O_AI:O_AI + 8], "w", [], [WG], slow=True)
            dma("pool", wg[:, :, 8:24], win_v[:, :, O_BF:O_BF + 16], "w", [], [WG], slow=True)
            dma("sp", bias_g[:, 0:4], abi_d.partition_broadcast(128), "c", [], [BG])
            dma("sp", bias_g[:, 4:8], abf_d.partition_broadcast(128), "c", [], [BG])
            dma("sp", bias_g[:, 8:24], bbf_d.partition_broadcast(128), "c", [], [BG])

            with ExitStack() as sA:
                xs = sbuf(sA, "xs", [128, NT, D], F32)
                XS = [Buf("xs%d" % t) for t in range(NT)]
                gbc = sbuf(sA, "gbc", [128, D], F32)
                GBC = Buf("gbc")
                ssq = sbuf(sA, "ssq", [128, NT], F32)
                rstd = sbuf(sA, "rstd", [128, NT], F32)
                SSQ = Buf("ssq")
                junk = sbuf(sA, "junk", [128, D], BF16)
                JUNK = Buf("junk")
                hb = [sbuf(sA, "hb%d" % i, [128, D], BF16) for i in range(2)]
                HB = [Buf("hb%d" % i) for i in range(2)]
                dma("sp", gbc[:], g_attn_d.partition_broadcast(128), "c", [], [GBC])
                for t in range(NT):
                    dma("sp", xs[:, t, :], x_d[t * 128:(t + 1) * 128, :], "x", [], [XS[t]])
                for k in range(4):
                    dma("sp", convw[:, :, k], convw_d[k:k + 1, :].rearrange("o (j p) -> p (o j)", p=128), "c", [], [CW], slow=True, prefetch=True)
                dma("sp", convb[:], convb_d.rearrange("o (j p) -> p (o j)", p=128), "c", [], [CW], slow=True, prefetch=True)
                SSQt = [Buf("ssq%d" % t) for t in range(NT)]
                S.op("dve", lambda e: e.memset(ssq[:], 0.0), w=SSQt)
                load_wC(0, [XS[15]], part="voz")
                def a_stats(t):
                    act(junk[:], xs[:, t, :], AF.Square, [XS[t]], [JUNK, SSQt[t]], accum_out=ssq[:, t:t + 1])
                    act(rstd[:, t:t + 1], ssq[:, t:t + 1], AF.Sqrt, [], [SSQt[t]], scale=1.0 / D, bias=EPS)
                    recip(rstd[:, t:t + 1], rstd[:, t:t + 1], [], [SSQt[t]])

                def a_norm(t):
                    b = t % 2
                    stt(hb[b][:], xs[:, t, :], rstd[:, t:t + 1], gbc[:], ALU.mult, ALU.mult, [XS[t], SSQt[t], GBC], [HB[b]])
                    fns = [tr(T[b][:, kc * 128:(kc + 1) * 128], hb[b][:, kc * 128:(kc + 1) * 128], ident[:]) for kc in range(8)]
                    S.group("pe", fns, [HB[b], CONST], [TB_[b]])

                def a_evac(t):
                    b = t % 2
                    cp(hT[:, :, t * 128:(t + 1) * 128], T[b][:, :].rearrange("p (k n) -> p k n", k=8), [TB_[b]], [HT[t]])

                a_stats(0)
                for t in range(NT + 1):
                    if t + 1 < NT:
                        a_stats(t + 1)
                    if t < NT:
                        a_norm(t)
                    if t >= 1:
                        a_evac(t - 1)
                load_wC(1, [XS[15]])
            S.barrier()
            if debug:
                dbg_tickets.append(dma("sp", dbg["hT"], hT[:], "d", HT, []))

            if STOP_AFTER == 'A':
                S.muted = True
            with ExitStack() as sC:
                sB = sC
                gpre = sbuf(sB, "gpre", [128, 16, 24], F32)
                GP = Buf("gpre")
                lf = sbuf(sB, "lf", [128, 16, 20], F32)
                lr = sbuf(sB, "lr", [128, 16, 20], F32)
                LF = Buf("lf")
                parts = [sbuf(sB, "lfp%d" % i, [128, 320], BF16) for i in range(3)]
                PR = Buf("parts")
                bcs = sbuf(sB, "bcs", [128, 16, 20], F32)
                tot = sbuf(sB, "tot", [128, 16, 20], F32)
                BC = Buf("bcs")
                tmpa = sbuf(sB, "tmpa", [128, 16, 4], F32)
                tmpb = sbuf(sB, "tmpb", [128, 16, 4], F32)
                TM = Buf("tmpab")
                def phaseB_part1():
                    for t in range(NT):
                        acc_group(P[0][:, t * 24:(t + 1) * 24],
                                  [(hT[:, kc, t * 128:(t + 1) * 128], wg[:, kc, :]) for kc in range(8)],
                                  [HT[t], WG], [PB_[0]])
                    for t in range(NT):
                        tt(gpre[:, t, :], P[0][:, t * 24:(t + 1) * 24], bias_g[:], ALU.add, [PB_[0], BG], [GP])
                    if debug:
                        dbg_tickets.append(dma("sp", dbg["gates"], gpre[:], "d", [GP], []))
                    act(lf[:], gpre[:, :, 4:24], AF.Exp, [GP], [LF], scale=-1.0)
                    act(lf[:], lf[:], AF.Ln, [], [LF], bias=1.0)
                    ts(lf[:], lf[:], -1.0, None, ALU.mult, None, [], [LF])
                    lf2 = lf[:].rearrange("p t g -> p (t g)")
                    lr2 = lr[:].rearrange("p t g -> p (t g)")
                    cp(parts[0][:], lf2, [LF], [PR])
                    tt(lr2, lf2, parts[0][:], ALU.subtract, [PR], [LF])
                    cp(parts[1][:], lr2, [LF], [PR])
                    tt(lr2, lr2, parts[1][:], ALU.subtract, [PR], [LF])
                    cp(parts[2][:], lr2, [LF], [PR])

                def phaseB_part2():
                    acc_group(P[1][:, 0:320], [(maskT[:], parts[i][:]) for i in range(3)], [PR, CONST], [PB_[1]])
                    acc_group(P[2][:, 0:320], [(ones_bf[:], parts[i][:]) for i in range(3)], [PR, CONST], [PB_[2]])
                    act(bcs[:].rearrange("p t g -> p (t g)"), P[1][:, 0:320], AF.Copy, [PB_[1]], [BC])
                    act(tot[:].rearrange("p t g -> p (t g)"), P[2][:, 0:320], AF.Copy, [PB_[2]], [BC])
                    cq = float(128 ** -0.5)
                    tt(tmpa[:], gpre[:, :, 0:4], bcs[:, :, 0:4], ALU.subtract, [GP, BC], [TM])
                    tt(tmpb[:], tmpa[:], tot[:, :, 0:4], ALU.add, [BC], [TM])
                    act(m_ek[:], tmpa[:], AF.Exp, [TM], [SCAL])
                    act(m_eke[:], tmpb[:], AF.Exp, [TM], [SCAL])
                    act(m_eq[:], bcs[:, :, 0:4], AF.Exp, [BC], [SCAL])
                    act(m_eg[:], tot[:, :, 0:4], AF.Exp, [BC], [SCAL])
                    ts(m_ek[:], m_ek[:], cq, None, ALU.mult, None, [], [SCAL])
                    ts(m_eke[:], m_eke[:], cq, None, ALU.mult, None, [], [SCAL])
                    S.op("dve", lambda e: e.memset(f_carry[:, 0, :], 0.0), w=[SCAL])
                    for j in range(NT):
                        tt(f_carry[:, j + 1, :], f_carry[:, j, :], tot[:, j, 4:20], ALU.add, [BC], [SCAL])
                    tt(f_cum[:], bcs[:, :, 4:20], f_carry[:, 0:16, :], ALU.add, [BC], [SCAL])
                    if debug:
                        dbg_tickets.append(dma("sp", dbg["cum"], f_cum[:], "d", [SCAL], []))


                pre_bf = [sbuf(sC, "pre_bf%d" % i, [128, 2052], BF16) for i in range(2)]
                PRE = [[Buf("pre%d_%d" % (i, tb)) for tb in range(4)] for i in range(2)]
                diagw = sbuf(sC, "diagw", [128, 8, 4, 128], BF16)
                DGW = Buf("diagw")
                hraw = mergedT[:, 0:4, :].rearrange("p a b -> p (a b)").bitcast(F32).rearrange("p (t f) -> p t f", t=16)
                HR = Buf("hraw")
                qT = sbuf(sC, "qT", [128, S_LEN], BF16)
                kT = sbuf(sC, "kT", [128, S_LEN], BF16)
                QK = [Buf("qTb"), Buf("kTb")]
                v_aug = sbuf(sC, "v_aug", [128, NT, 258], BF16)
                VA = [Buf("v_aug%d" % t) for t in range(NT)]
                Gh = sbuf(sC, "Gh", [128, NT, 256], BF16)
                GH = [Buf("Gh%d" % t) for t in range(NT)]
                ks_end = sbuf(sC, "ks_end", [128, NT, 128], BF16)
                KE = [Buf("ks_end%d" % t) for t in range(NT)]
                Cbf = sbuf(sC, "Cbf", [128, NT, 258], BF16)
                CB = [Buf("Cbf%d" % t) for t in range(NT)]
                Cf = [sbuf(sC, "Cf%d" % i, [128, 257], F32) for i in range(2)]
                CF = [Buf("Cf%d" % i) for i in range(2)]
                ghn = sbuf(sC, "ghn", [128, 1024], F32)
                GHN = Buf("ghn")
                to_t = [sbuf(sC, "to%d" % i, [128, 256], F32) for i in range(2)]
                sz_t = [sbuf(sC, "sz%d" % i, [128, 256], F32) for i in range(2)]
                TO = [Buf("to%d" % i) for i in range(2)]
                SZB = [Buf("sz%d" % i) for i in range(2)]
                sT = [sbuf(sC, "sT%d" % i, [128, 128], BF16) for i in range(2)]
                STB = [Buf("sT%d" % i) for i in range(2)]
                hag = [sbuf(sC, "hag%d" % i, [128, 256], BF16) for i in range(2)]
                HG = [Buf("hag%d" % i) for i in range(2)]
                junkc = sbuf(sC, "junkc", [128, 256], BF16)
                JC = Buf("junkc")
                ssr = sbuf(sC, "ssr", [128, NT], F32)
                den = sbuf(sC, "den", [128, NT], F32)
                fac = sbuf(sC, "fac", [128, NT], F32)
                stot = sbuf(sC, "stot", [128, NT], F32)
                tmps = sbuf(sC, "tmps", [128, NT], F32)
                SS = Buf("ssr")
                ST_ = Buf("stot")

                phaseB_part1()
                dma("sp", ghn[:], ghn_d.partition_broadcast(128), "c", [], [GHN])
                for i in range(2):
                    S.op("dve", lambda e, i=i: e.memset(pre_bf[i][:, 0:3], 0.0), w=[PRE[i][0]])
                for j in range(8):
                    for k in range(4):
                        wl = [DGW] if (j, k) in ((0, 0), (7, 3)) else []
                        ts(diagw[:, j, k, :], ident[:], convw[:, j, k:k + 1], None, ALU.mult, None, [CW, CONST], wl)
                ts(ghn[:], ghn[:], 0.5, None, ALU.mult, None, [], [GHN])
                S.op("pool", lambda e: e.memset(v_aug[:, :, 256:258], 1.0), w=VA)
                projP = Rot([0, 1])
                trR = Rot([0, 1])

                def ktr_tile(h, t):
                    b = trR.next()
                    S.group("pe", [tr(T[b][:, 0:128], kT[:, t * 128:(t + 1) * 128], ident[:])], [QK[1], CONST], [TB_[b]])
                    ts(ks_end[:, t, :], T[b][:, 0:128], m_eke[:, t, h:h + 1], None, ALU.mult, None, [TB_[b], SCAL], [KE[t]])

                def stage_qk(h, with_ktr=False):
                    wt = wts[h % 2]
                    WB_ = WTQK[h % 2]

                    def proj(wi, tb):
                        coff = 128 * wi
                        pi = projP.next()
                        acc_group(P[pi][:, :], [(wt[:, kc, coff:coff + 128], hT[:, kc, tb * 512:(tb + 1) * 512]) for kc in range(8)],
                                  [WB_] + HT[tb * 4:tb * 4 + 4], [PB_[pi]])
                        act(pre_bf[wi][:, 3 + tb * 512:3 + (tb + 1) * 512], P[pi][:, :], AF.Copy, [PB_[pi]], [PRE[wi][tb]])

                    def conv(wi, tb):
                        dst = (qT, kT)[wi]
                        j = wi * 4 + h
                        pi = projP.next()
                        rds = [DGW, PRE[wi][tb]] + ([PRE[wi][tb - 1]] if tb > 0 else [])
                        acc_group(P[pi][:, :], [(diagw[:, j, k, :], pre_bf[wi][:, tb * 512 + k:tb * 512 + k + 512]) for k in range(4)],
                                  rds, [PB_[pi]])
                        act(dst[:, tb * 512:(tb + 1) * 512], P[pi][:, :], AF.Silu, [PB_[pi], CW], [QK[wi]], bias=convb[:, j:j + 1])

                    for tb in range(4):
                        proj(1, tb)
                    for tb in range(4):
                        conv(1, tb)
                    for tb in range(4):
                        proj(0, tb)
                        if with_ktr:
                            for t in range(tb * 4, tb * 4 + 4):
                                ktr_tile(h, t)
                    for tb in range(4):
                        conv(0, tb)

                def stage_ktr(h):
                    for t in range(NT):
                        b = trR.next()
                        S.group("pe", [tr(T[b][:, 0:128], kT[:, t * 128:(t + 1) * 128], ident[:])], [QK[1], CONST], [TB_[b]])
                        ts(ks_end[:, t, :], T[b][:, 0:128], m_eke[:, t, h:h + 1], None, ALU.mult, None, [TB_[b], SCAL], [KE[t]])

                def emit_proj(h, t):
                    wt = wts[h % 2]
                    WB_ = WT[h % 2]
                    pv = projP.next()
                    acc_group(P[pv][:, 0:256], [(hT[:, kc, t * 128:(t + 1) * 128], wt[:, kc, 256:512]) for kc in range(8)],
                              [WB_, HT[t]], [PB_[pv]])
                    po = projP.next()
                    acc_group(P[po][:, 0:512], [(hT[:, kc, t * 128:(t + 1) * 128], wt[:, kc, 512:1024]) for kc in range(8)],
                              [WB_, HT[t]], [PB_[po]])

                    def ev():
                        act(v_aug[:, t, 0:256], P[pv][:, 0:256], AF.Copy, [PB_[pv]], [VA[t]])
                        b = t % 2
                        act(to_t[b][:], P[po][:, 0:256], AF.Tanh, [PB_[po]], [TO[b]], scale=0.5)
                        act(sz_t[b][:], P[po][:, 256:512], AF.Silu, [PB_[po]], [SZB[b]])
                        stt(to_t[b][:], to_t[b][:], 1.0, ghn[:, h * 256:(h + 1) * 256], ALU.add, ALU.mult, [GHN], [TO[b]])
                        tt(Gh[:, t, :], to_t[b][:], sz_t[b][:], ALU.mult, [TO[b], SZB[b]], [GH[t]], eng="pool")
                    return ev

                def stage_loop(h, pre=None):
                    S.op("dve", lambda e: e.memset(ssr[:], 0.0), w=[SS])
                    emit_proj(h, 0)()
                    done1 = False
                    if pre is not None:
                        emit_proj(h, 1)()
                        done1 = True
                        pre()
                    for t in range(NT):
                        b = t % 2
                        if t < NT - 1:
                            acc_group(P[4][:, 0:257], [(ks_end[:, t, :], v_aug[:, t, 0:257])], [KE[t], VA[t]], [PB_[4]])
                        acc_group(P[2][:, 0:128], [(kT[:, t * 128:(t + 1) * 128], qT[:, t * 128:(t + 1) * 128])], [QK[0], QK[1]], [PB_[2]])
                        ev = emit_proj(h, t + 1) if (t + 1 < NT and not (t == 0 and done1)) else None
                        if t < NT - 1:
                            cb = t % 2
                            if t == 0:
                                cp(Cf[cb][:], P[4][:, 0:257], [PB_[4]], [CF[cb]])
                            else:
                                stt(Cf[cb][:], Cf[1 - cb][:], m_eg[:, t, h:h + 1], P[4][:, 0:257], ALU.mult, ALU.add,
                                    [PB_[4], CF[1 - cb], SCAL], [CF[cb]])
                            act(Cbf[:, t, 0:257], Cf[cb][:], AF.Copy, [CF[cb]], [CB[t]])
                        stt(sT[b][:], P[2][:, 0:128], m_ek[:, t, h:h + 1], maskT[:], ALU.mult, ALU.mult, [PB_[2], SCAL, CONST], [STB[b]])
                        if ev is not None:
                            ev()
                        pairs = []
                        rds = [STB[b], VA[t], QK[0]]
                        if t > 0:
                            pairs.append((qT[:, t * 128:(t + 1) * 128], Cbf[:, t - 1, 0:257]))
                            rds.append(CB[t - 1])
                        pairs.append((sT[b][:], v_aug[:, t, 0:257]))
                        acc_group(P[3][:, 0:257], pairs, rds, [PB_[3]])
                        act(junkc[:], P[3][:, 0:256], AF.Square, [PB_[3]], [JC, SS], accum_out=ssr[:, t:t + 1])
                        cp(den[:, t:t + 1], P[3][:, 256:257], [PB_[3]], [SS])
                        cp(hraw[:, t, :], P[3][:, 0:256], [PB_[3]], [HR])

                def stage_final(h):
                    tt(tmps[:], den[:], m_eq[:, :, h], ALU.mult, [SCAL, SS], [ST_])
                    stt(tmps[:], tmps[:], -1.0, tmps[:], ALU.mult, ALU.max, [], [ST_])
                    ts(tmps[:], tmps[:], 1.0, None, ALU.max, None, [], [ST_])
                    S.op("dve", lambda e: e.reciprocal(out=tmps[:], in_=tmps[:]), w=[ST_])
                    tt(fac[:], tmps[:], m_eq[:, :, h], ALU.mult, [SCAL], [ST_])
                    tt(tmps[:], ssr[:], fac[:], ALU.mult, [SS], [ST_])
                    tt(tmps[:], tmps[:], fac[:], ALU.mult, [], [ST_])
                    act(tmps[:], tmps[:], AF.Sqrt, [], [ST_], scale=1.0 / 256, bias=EPS)
                    S.op("dve", lambda e: e.reciprocal(out=tmps[:], in_=tmps[:]), w=[ST_])
                    tt(stot[:], tmps[:], fac[:], ALU.mult, [], [ST_])
                    for t in range(NT):
                        b = t % 2
                        stt(hag[b][:], hraw[:, t, :], stot[:, t:t + 1], Gh[:, t, :], ALU.mult, ALU.mult, [HR, ST_, GH[t]], [HG[b]])
                        tb_i = trR.next()
                        S.group("pe", [tr(T[tb_i][:, 0:128], hag[b][:, 0:128], ident[:]), tr(T[tb_i][:, 128:256], hag[b][:, 128:256], ident[:])],
                                [HG[b], CONST], [TB_[tb_i]])
                        act(haGT[:, 2 * h:2 * h + 2, t * 128:(t + 1) * 128], T[tb_i][:, 0:256].rearrange("p (k n) -> p k n", k=2),
                            AF.Copy, [TB_[tb_i]], [HA])

                stage_qk(0)
                phaseB_part2()
                stage_loop(0, pre=lambda: stage_ktr(0))
                for h in range(1, 4):
                    if h + 1 < 4:
                        load_wC(h + 1)
                    if h == 3:
                        for i, off in enumerate((O_BQ, O_BK, O_BV, O_BZ)):
                            dma("pool", wtA[:, :, i * 128:(i + 1) * 128], win_v[:, :, off:off + 128], "w", [], [WT[0], WTQK[0]], prefetch=True)
                    stage_qk(h, with_ktr=True)
                    stage_final(h - 1)
                    stage_loop(h)
                stage_final(3)
            S.barrier()
            if debug:
                dbg_tickets.append(dma("sp", dbg["haGT"], haGT[:], "d", [HA], []))

            hbGT = sbuf(sAE, "hbGT", [128, 8, S_LEN], BF16)
            HBG = Buf("hbGT")

            with ExitStack() as sD:
                wtD = [sbuf(sD, "wtD%d" % i, [128, 8, 512], BF16) for i in range(2)]
                WTD = [Buf("wtD%d" % i) for i in range(2)]
                qhA = [sbuf(sD, "qh%d" % i, [128, S_LEN], BF16) for i in range(2)]
                khA = [sbuf(sD, "kh%d" % i, [128, S_LEN], BF16) for i in range(2)]
                VpA = sbuf(sD, "Vp", [128, NT, 192], BF16)
                SZA = sbuf(sD, "SZ", [128, S_LEN], BF16)
                mflat = mergedT[:, 4:6, :].rearrange("p a b -> p (a b)")
                sets = [
                    dict(qh=[qhA[0][:, :], qhA[1][:, :]], kh=[khA[0][:, :], khA[1][:, :]], Vp=VpA[:, :, :], SZ=SZA[:, :]),
                    dict(qh=[mergedT[:, 0, :], mergedT[:, 1, :]], kh=[mergedT[:, 2, :], mergedT[:, 3, :]],
                         Vp=mflat[:, 0:3072].rearrange("p (t c) -> p t c", c=192), SZ=mergedT[:, 6, :]),
                ]
                for i, st_ in enumerate(sets):
                    st_["QD"] = [Buf("qh%d_%d" % (i, k)) for k in range(2)]
                    st_["KD"] = [Buf("kh%d_%d" % (i, k)) for k in range(2)]
                    st_["VP"] = Buf("Vp%d" % i)
                    st_["SZD"] = Buf("SZ%d" % i)
                tz = [sbuf(sD, "tz%d" % i, [128, 512], F32) for i in range(2)]
                TZ = [Buf("tz%d" % i) for i in range(2)]
                PT = [sbuf(sD, "PT%d" % i, [128, 512], BF16) for i in range(6)]
                PTB = [Buf("PT%d" % i) for i in range(6)]
                negcum = sbuf(sD, "negcum", [128, 16, 16], F32)
                carry8 = sbuf(sD, "carry8", [128, 16, 16], F32)
                BD = Buf("biasD")
                rd = [sbuf(sD, "rd%d" % i, [128, 512], F32) for i in range(2)]
                RD = [Buf("rd%d" % i) for i in range(2)]
                ts(negcum[:], f_cum[:], -1.0, None, ALU.mult, None, [SCAL], [BD])
                ts(carry8[:], f_carry[:, 0:16, :], 8.0, None, ALU.mult, None, [SCAL], [BD])

                def memset(eng, ap, val, w):
                    return S.op(eng, lambda e: e.memset(ap, val), w=w)

                def load_wD(pr):
                    w_ = wtD[pr % 2]
                    for i, off in enumerate((O_BQ, O_BK, O_BV, O_BZ)):
                        dma("pool", w_[:, :, i * 128:(i + 1) * 128], win_v[:, :, off + 128 * pr:off + 128 * (pr + 1)], "w", [], [WTD[pr % 2]])

                for si, st_ in enumerate(sets):
                    e0, e1 = ("dve", "pool") if si == 0 else ("pool", "dve")
                    memset("pool", st_["Vp"][:, :, 64:128], 2.0, [st_["VP"]])
                    for k in range(2):
                        memset(e0, st_["qh"][k][64:128, :], 0.0, [st_["QD"][k]])
                        memset(e1, st_["kh"][k][64:128, :], 0.0, [st_["KD"][k]])
                        memset(e1, st_["kh"][k][64:65, :], 1.0, [st_["KD"][k]])

                projP = Rot([0, 1])
                sP = Rot([2, 3, 7])
                oP = Rot([4, 5, 6])
                ptR = Rot([0, 1, 2, 3, 4, 5])
                tzR = Rot([0, 1])
                rdR = Rot([0, 1])
                norm_q = []

                def proj_units(pr):
                    st_ = sets[pr % 2]
                    w_ = wtD[pr % 2] if pr > 0 else wtA[:, :, 0:512]
                    WB_ = WTD[pr % 2] if pr > 0 else WT[0]
                    units = []

                    def u_q(tb):
                        hts = HT[tb * 4:tb * 4 + 4]
                        tsl = slice(tb * 512, (tb + 1) * 512)
                        pi = projP.next()
                        acc_group(P[pi][:, :], [(w_[:, kc, 0:128], hT[:, kc, tsl]) for kc in range(8)], [WB_] + hts, [PB_[pi]])

                        def ev():
                            cp(st_["qh"][0][0:64, tsl], P[pi][0:64, :], [PB_[pi]], [st_["QD"][0]])
                            act(st_["qh"][1][0:64, tsl], P[pi][64:128, :], AF.Copy, [PB_[pi]], [st_["QD"][1]])
                        return ev

                    def u_k(tb):
                        hts = HT[tb * 4:tb * 4 + 4]
                        tsl = slice(tb * 512, (tb + 1) * 512)
                        pi = projP.next()
                        acc_group(P[pi][:, :], [(w_[:, kc, 128:256], hT[:, kc, tsl]) for kc in range(8)], [WB_] + hts, [PB_[pi]])

                        def ev():
                            cp(st_["kh"][0][0:64, tsl], P[pi][0:64, :], [PB_[pi]], [st_["KD"][0]])
                            cp(st_["kh"][1][0:64, tsl], P[pi][64:128, :], [PB_[pi]], [st_["KD"][1]])
                        return ev

                    def u_z(tb):
                        hts = HT[tb * 4:tb * 4 + 4]
                        tsl = slice(tb * 512, (tb + 1) * 512)
                        pi = projP.next()
                        acc_group(P[pi][:, :], [(w_[:, kc, 384:512], hT[:, kc, tsl]) for kc in range(8)], [WB_] + hts, [PB_[pi]])

                        def ev():
                            b = tzR.next()
                            act(tz[b][:], P[pi][:, :], AF.Tanh, [PB_[pi]], [TZ[b]], scale=0.5)
                            stt(st_["SZ"][:, tsl], tz[b][:], 1.0, P[pi][:, :], ALU.add, ALU.mult, [TZ[b], PB_[pi]], [st_["SZD"]])
                        return ev

                    def u_shift(hh):
                        head = 2 * pr + hh

                        def ev():
                            for bq in range(NT):
                                wl = [st_["QD"][hh]] if bq in (0, NT - 1) else []
                                ts(st_["qh"][hh][64:65, bq * 128:(bq + 1) * 128], ones_bf[64:65, :], carry8[64:65, bq, head:head + 1], None,
                                   ALU.mult, None, [BD, CONST], wl)
                        return ev

                    def u_v(t):
                        pi = projP.next()
                        acc_group(P[pi][:, 0:128], [(hT[:, kc, t * 128:(t + 1) * 128], w_[:, kc, 256:384]) for kc in range(8)], [WB_, HT[t]], [PB_[pi]])

                        def ev():
                            act(st_["Vp"][:, t, :].rearrange("p (a b) -> p a b", b=64)[:, 0:3:2, :],
                                P[pi][:, 0:128].rearrange("p (a b) -> p a b", b=64), AF.Copy, [PB_[pi]], [st_["VP"]])
                        return ev

                    for tb in range(4):
                        units.append(lambda tb=tb: u_q(tb))
                        units.append(lambda tb=tb: u_k(tb))
                        units.append(lambda tb=tb: u_z(tb))
                    for hh in range(2):
                        units.append(lambda hh=hh: u_shift(hh))
                    for t in range(NT):
                        units.append(lambda t=t: u_v(t))
                    return units

                for u in proj_units(0):
                    u()()
                WAB = Buf("wab")
                for pr in range(8):
                    st_ = sets[pr % 2]
                    pending = []
                    if pr == 1:
                        dma("pool", wtA[:], wa_d.rearrange("(kc p) n -> p kc n", p=128), "w", [], [WAB, WT[0], WTQK[0]])
                    if pr + 1 < 8:
                        load_wD(pr + 1)
                        pending = proj_units(pr + 1)
                    nsteps = 76
                    step_i = 0
                    emitted = 0
                    deferred = []
                    for hh in range(2):
                        head = 2 * pr + hh
                        qh_, kh_ = st_["qh"][hh], st_["kh"][hh]
                        QD_, KD_ = st_["QD"][hh], st_["KD"][hh]
                        Vp_, VP_ = st_["Vp"], st_["VP"]
                        vcols = slice(0, 128) if hh == 0 else slice(64, 192)
                        nrows = slice(0, 64) if hh == 0 else slice(64, 128)
                        drows = slice(64, 128) if hh == 0 else slice(0, 64)
                        for g in range(4):
                            po = oP.next()
                            nj = 4 * g + 4

                            def emit_mm1(j):
                                ps_i = sP.next()
                                q0 = max(4 * g, j) * 128
                                q1 = (4 * g + 4) * 128
                                c0 = q0 - 4 * g * 128
                                if j >= 4 * g:
                                    fns = [mm(P[ps_i][:, c0:512], kh_[:, j * 128:(j + 1) * 128], qh_[:, q0:q1], True, False),
                                           mm(P[ps_i][:, c0:c0 + 128], ident[:], negmask[:], False, True)]
                                    S.group("pe", fns, [QD_, KD_, CONST], [PB_[ps_i]])
                                else:
                                    acc_group(P[ps_i][:, c0:512], [(kh_[:, j * 128:(j + 1) * 128], qh_[:, q0:q1])], [QD_, KD_], [PB_[ps_i]])
                                return ps_i

                            LOOK = 2
                            banks = [emit_mm1(jj) for jj in range(min(LOOK, nj))]
                            for j in range(nj):
                                cur = banks[j]
                                pti = ptR.next()
                                b0 = max(4 * g, j)
                                c0 = (b0 - 4 * g) * 128
                                act(PT[pti][:, c0:512], P[cur][:, c0:512], AF.Exp, [PB_[cur], BD], [PTB[pti]],
                                    scale=0.125, bias=negcum[:, j, head:head + 1])
                                for ev in deferred:
                                    ev()
                                deferred = []
                                if norm_q and j >= 1:
                                    norm_q.pop(0)()
                                if j + LOOK < nj:
                                    banks.append(emit_mm1(j + LOOK))
                                step_i += 1
                                want = (len(pending) * step_i + nsteps - 1) // nsteps if pending else 0
                                while emitted < min(want, len(pending)):
                                    deferred.append(pending[emitted]())
                                    emitted += 1
                                fn = mm(P[po][:, c0:512], Vp_[:, j, vcols], PT[pti][:, c0:512], j == 0, j == nj - 1)
                                S.group("pe", [fn], [VP_, PTB[pti]], [PB_[po]])
                            def norm_piece(qq, po=po, g=g, nrows=nrows, drows=drows, SZ_=st_["SZ"], SZD_=st_["SZD"], pr=pr):
                                b = rdR.next()
                                cs = slice(qq * 256, (qq + 1) * 256)
                                gs = slice(g * 512 + qq * 256, g * 512 + (qq + 1) * 256)
                                recip(rd[b][nrows, cs], P[po][drows, cs], [PB_[po]], [RD[b]])
                                tt(rd[b][nrows, cs], rd[b][nrows, cs], SZ_[nrows, gs], ALU.mult, [SZD_], [RD[b]])
                                tt(hbGT[nrows, pr, gs], P[po][nrows, cs], rd[b][nrows, cs], ALU.mult, [PB_[po], RD[b]], [HBG])
                            norm_piece(0)
                            norm_q.append(lambda f=norm_piece: f(1))
                    for ev in deferred:
                        ev()
                    while norm_q:
                        norm_q.pop(0)()
                    while emitted < len(pending):
                        pending[emitted]()()
                        emitted += 1
            S.barrier()
            if debug:
                dbg_tickets.append(dma("sp", dbg["hbGT"], hbGT[:], "d", [HBG], []))

            wo = sbuf(sAE, "wo", [128, 8, D], BF16)
            wpp = sbuf(sAE, "wpp", [128, 2, D], BF16)
            WO = [Buf("wo%d" % i) for i in range(2)]
            WPP = Buf("wpp")
            with ExitStack() as sE:
                wa = wtA
                wb = sbuf(sE, "wb", [128, 8, D], BF16)
                wgE = [sbuf(sE, "wgE%d" % i, [128, 8, 256], BF16) for i in range(2)]
                WGE = [Buf("wgE%d" % i) for i in range(2)]
                ta = [sbuf(sE, "ta%d" % i, [128, 512], F32) for i in range(2)]
                tb_ = [sbuf(sE, "tb%d" % i, [128, 512], F32) for i in range(2)]
                TA = [Buf("ta%d" % i) for i in range(2)]
                TBb = [Buf("tb%d" % i) for i in range(2)]
                WBc = [Buf("wb%d" % c) for c in range(8)]
                wb_v = wb_d.rearrange("(kc p) n -> p kc n", p=128)

                def load_wE(c):
                    dma("pool", wgE[c % 2][:, :, 0:128], win_v[:, :, O_GA + 128 * c:O_GA + 128 * (c + 1)], "w", [], [WGE[c % 2]])
                    dma("pool", wgE[c % 2][:, :, 128:256], win_v[:, :, O_GB + 128 * c:O_GB + 128 * (c + 1)], "w", [], [WGE[c % 2]])

                load_wE(0)
                for c in range(8):
                    dma("pool", wb[:, :, c * 128:(c + 1) * 128], wb_v[:, :, c * 128:(c + 1) * 128], "w", [], [WBc[c]])
                it = 0
                wo_v = wo_d.rearrange("(kc p) n -> p kc n", p=128)
                for c in range(8):
                    if c + 1 < 8:
                        load_wE(c + 1)
                    if c == 2:
                        for i in range(2):
                            dma("pool", wo[:, :, i * 512:(i + 1) * 512], wo_v[:, :, i * 512:(i + 1) * 512], "w", [], [WO[i]], prefetch=True)
                        dma("pool", wpp[:], wpp_d.rearrange("(kc p) n -> p kc n", p=128), "w", [], [WPP], prefetch=True)
                    wg_ = wgE[c % 2]
                    for tb in range(4):
                        hts = HT[tb * 4:tb * 4 + 4]
                        tsl = slice(tb * 512, (tb + 1) * 512)
                        b = it % 2
                        it += 1
                        pga, pgb = (0, 1) if b == 0 else (4, 5)
                        acc_group(P[pga][:, :], [(wg_[:, kc, 0:128], hT[:, kc, tsl]) for kc in range(8)], [WGE[c % 2]] + hts, [PB_[pga]])
                        acc_group(P[pgb][:, :], [(wg_[:, kc, 128:256], hT[:, kc, tsl]) for kc in range(8)], [WGE[c % 2]] + hts, [PB_[pgb]])
                        act(ta[b][:], P[pga][:, :], AF.Tanh, [PB_[pga]], [TA[b]], scale=0.5)
                        act(tb_[b][:], P[pgb][:, :], AF.Tanh, [PB_[pgb]], [TBb[b]], scale=0.5)
                        acc_group(P[2][:, :], [(wa[:, kc, c * 128:(c + 1) * 128], haGT[:, kc, tsl]) for kc in range(8)], [WAB, HA], [PB_[2]])
                        acc_group(P[3][:, :], [(wb[:, kc, c * 128:(c + 1) * 128], hbGT[:, kc, tsl]) for kc in range(8)], [WBc[c], HBG], [PB_[3]])
                        stt(ta[b][:], ta[b][:], 1.0, P[2][:, :], ALU.add, ALU.mult, [PB_[2]], [TA[b]])
                        stt(tb_[b][:], tb_[b][:], 1.0, P[3][:, :], ALU.add, ALU.mult, [PB_[3]], [TBb[b]])
                        tt(mergedT[:, c, tsl], ta[b][:], tb_[b][:], ALU.add, [TA[b], TBb[b]], [MT[tb]], eng="pool")
            S.barrier()
            if debug:
                dbg_tickets.append(dma("sp", dbg["mT"], mergedT[:], "d", MT, []))

            S.barrier()
            with ExitStack() as sF:
                wpg = wtA
                WF = Buf("wF")
                _hv = hT[:, :, :].rearrange("p a b -> p (a b)").bitcast(F32).rearrange("p (t d) -> p t d", d=D)
                _av = haGT[:, :, :].rearrange("p a b -> p (a b)").bitcast(F32).rearrange("p (t d) -> p t d", d=D)
                x1t = [_hv[:, t, :] for t in range(8)] + [_av[:, t, :] for t in range(8)]
                X1 = [Buf("x1_%d" % t) for t in range(NT)]
                xin = [hbGT[:, 4 + i, :].bitcast(F32) for i in range(2)]
                XIN = [Buf("xin%d" % i) for i in range(2)]
                pin = [sbuf(sF, "pin%d" % i, [128, 256], F32) for i in range(2)]
                PIN = [Buf("pin%d" % i) for i in range(2)]
                pbf = [sbuf(sF, "pbf%d" % i, [128, 256], BF16) for i in range(3)]
                PBF = [Buf("pbf%d" % i) for i in range(3)]
                pT = [sbuf(sF, "pT%d" % i, [128, 2, 128], BF16) for i in range(2)]
                PTF = [Buf("pTf%d" % i) for i in range(2)]
                gple = sbuf(sF, "gple", [128, D], F32)
                gfin = sbuf(sF, "gfin", [128, D], F32)
                GF = Buf("gF")
                ss1 = sbuf(sF, "ss1", [128, NT], F32)
                rs1 = sbuf(sF, "rs1", [128, NT], F32)
                ss2 = sbuf(sF, "ss2", [128, NT], F32)
                rs2 = sbuf(sF, "rs2", [128, NT], F32)
                SS1, SS2 = Buf("ss1"), Buf("ss2")
                junkf = sbuf(sF, "junkf", [128, D], BF16)
                JF = Buf("junkf")
                xn = [hbGT[:, 6, i * D:(i + 1) * D] for i in range(2)] + [hbGT[:, 7, 0:D]]
                XN = [Buf("xn%d" % i) for i in range(3)]
                xnT = [sbuf(sF, "xnT%d" % i, [128, 8, 128], BF16) for i in range(2)]
                XNT = [Buf("xnT%d" % i) for i in range(2)]
                tg = [hbGT[:, i, :].bitcast(F32) for i in range(2)]
                TG = [Buf("tg%d" % i) for i in range(2)]
                ob = [hbGT[:, 2 + i, :].bitcast(F32) for i in range(2)]
                OB = [Buf("ob%d" % i) for i in range(2)]
                WPG = Buf("wpg")
                dma("sp", gple[:], gple_d.partition_broadcast(128), "c", [], [GF])
                dma("sp", gfin[:], gfin_d.partition_broadcast(128), "c", [], [GF])
                wpg_v = wpg_d.rearrange("(kc p) n -> p kc n", p=128)
                for i in range(2):
                    dma("pool", wpg[:, :, i * 512:(i + 1) * 512], wpg_v[:, :, i * 512:(i + 1) * 512], "w", [], [WPG])
                S1 = [Buf("ss1_%d" % t) for t in range(NT)]
                S2 = [Buf("ss2_%d" % t) for t in range(NT)]
                S.op("dve", lambda e: e.memset(ss1[:], 0.0), w=S1)
                S.op("dve", lambda e: e.memset(ss2[:], 0.0), w=S2)
                P4b = [P[4][:, :].bitcast(BF16), P[5][:, :].bitcast(BF16)]
                outs = []

                def f_A1(t):
                    b = t % 2
                    for half in range(2):
                        pi = half
                        acc_group(P[pi][:, :], [(mergedT[:, kc, t * 128:(t + 1) * 128], wo[:, kc, half * 512:(half + 1) * 512]) for kc in range(8)],
                                  [MT[t // 4], WO[half]], [PB_[pi]])
                        stt(x1t[t][:, half * 512:(half + 1) * 512], P[pi][:, :], 0.5, xin[b][:, half * 512:(half + 1) * 512], ALU.mult, ALU.add,
                            [PB_[pi], XIN[b]], [X1[t]])

                def f_B1(t):
                    b = t % 2
                    b3 = t % 3
                    cp(pbf[b3][:], pin[b][:], [PIN[b]], [PBF[b3]], eng="pool")
                    stt(xn[b3][:], x1t[t][:, :], rs1[:, t:t + 1], gple[:], ALU.mult, ALU.mult, [X1[t], S1[t], GF], [XN[b3]])

                def f_B2(t):
                    b = t % 2
                    b3 = t % 3
                    fns = [tr(T[0][:, kc * 128:(kc + 1) * 128], xn[b3][:, kc * 128:(kc + 1) * 128], ident[:]) for kc in range(8)]
                    S.group("pe", fns, [XN[b3], CONST], [TB_[0]])
                    fns = [tr(T[1][:, kc * 128:(kc + 1) * 128], pbf[b3][:, kc * 128:(kc + 1) * 128], ident[:]) for kc in range(2)]
                    S.group("pe", fns, [PBF[b3], CONST], [TB_[1]])
                    cp(xnT[b][:], T[0][:, :].rearrange("p (k n) -> p k n", k=8), [TB_[0]], [XNT[b]])
                    act(pT[b][:], T[1][:, 0:256].rearrange("p (k n) -> p k n", k=2), AF.Copy, [TB_[1]], [PTF[b]])

                def f_tails(ta, tc):
                    if ta is not None:
                        act(junkf[:], x1t[ta][:, :], AF.Square, [X1[ta]], [JF, S1[ta]], accum_out=ss1[:, ta:ta + 1])
                        act(rs1[:, ta:ta + 1], ss1[:, ta:ta + 1], AF.Sqrt, [], [S1[ta]], scale=1.0 / D, bias=EPS)
                    if tc is not None:
                        act(junkf[:], x1t[tc][:, :], AF.Square, [X1[tc]], [JF, S2[tc]], accum_out=ss2[:, tc:tc + 1])
                        act(rs2[:, tc:tc + 1], ss2[:, tc:tc + 1], AF.Sqrt, [], [S2[tc]], scale=1.0 / D, bias=EPS)
                    if ta is not None:
                        recip(rs1[:, ta:ta + 1], rs1[:, ta:ta + 1], [], [S1[ta]])
                        f_B1(ta)
                    if tc is not None:
                        b = tc % 2
                        recip(rs2[:, tc:tc + 1], rs2[:, tc:tc + 1], [], [S2[tc]])
                        stt(ob[b][:], x1t[tc][:, :], rs2[:, tc:tc + 1], gfin[:], ALU.mult, ALU.mult, [X1[tc], S2[tc], GF], [OB[b]])

                def f_C1(t):
                    b = t % 2
                    for half in range(2):
                        hs = slice(half * 512, (half + 1) * 512)
                        pg_, pp_ = (2, 3) if half == 0 else (4, 5)
                        acc_group(P[pg_][:, :], [(xnT[b][:, kc, :], wpg[:, kc, hs]) for kc in range(8)], [XNT[b], WPG], [PB_[pg_]])
                        acc_group(P[pp_][:, :], [(pT[b][:, kc, :], wpp[:, kc, hs]) for kc in range(2)], [PTF[b], WPP], [PB_[pp_]])
                        act(tg[b][:, hs], P[pg_][:, :], AF.Tanh, [PB_[pg_]], [TG[b]], scale=0.5)
                        stt(tg[b][:, hs], tg[b][:, hs], 1.0, P[pp_][:, :], ALU.add, ALU.mult, [PB_[pp_]], [TG[b]])
                        stt(x1t[t][:, hs], tg[b][:, hs], 0.5, x1t[t][:, hs], ALU.mult, ALU.add, [TG[b]], [X1[t]])

                def f_out(t):
                    b = t % 2
                    outs.append(dma("sp", y_d[t * 128:(t + 1) * 128, :], ob[b][:], "o", [OB[b]], []))

                def f_xin(t):
                    b = t % 2
                    dma("sp", xin[b][:], x_d[t * 128:(t + 1) * 128, :], "x", [], [XIN[b]])

                def f_pin(t):
                    b = t % 2
                    dma("sp", pin[b][:], p_d[t * 128:(t + 1) * 128, :], "x", [], [PIN[b]])

                f_xin(0)
                for t in range(NT + 5):
                    if t + 1 < NT:
                        f_xin(t + 1)
                    if t < NT:
                        f_pin(t)
                    if 0 <= t - 5 < NT:
                        f_out(t - 5)
                    if t < NT:
                        f_A1(t)
                    if 0 <= t - 2 < NT:
                        f_B2(t - 2)
                    f_tails(t if t < NT else None, (t - 4) if 0 <= t - 4 < NT else None)
                    if 0 <= t - 3 < NT:
                        f_C1(t - 3)
                S.muted = False
                S.op("sp", lambda e: e.nop(), extra=[t_ for t_ in outs + dbg_tickets if t_ is not None])

        if PRUNE:
            S.finalize()
        with nc.Block() as block:
            @block.sync
            def _(e):
                S.replay("sp", e)

            @block.scalar
            def _(e):
                S.replay("act", e)

            @block.vector
            def _(e):
                S.replay("dve", e)

            @block.gpsimd
            def _(e):
                S.replay("pool", e)

            @block.tensor
            def _(e):
                S.replay("pe", e)
    return nc


_NC_CACHE = {}


def _get_nc(debug=False):
    if debug not in _NC_CACHE:
        _NC_CACHE[debug] = build_nc(debug)
    return _NC_CACHE[debug]


def make_in_maps(inputs, n=8):
    f = lambda a: np.ascontiguousarray(np.asarray(a, dtype=np.float32))
    x = f(inputs["x"])
    p = f(inputs["p"])[0]
    shared = {
        "attn_norm_g": f(inputs["attn_norm_g"]).reshape(1, D),
        "w_in": f(inputs["w_in"])[0],
        "conv_w": f(inputs["conv_w"])[0],
        "conv_b": f(inputs["conv_b"]).reshape(1, 1024),
        "a_bias_i": f(inputs["a_bias_i"]).reshape(1, 4),
        "a_bias_f": f(inputs["a_bias_f"]).reshape(1, 4),
        "a_head_norm_g": f(inputs["a_head_norm_g"]).reshape(1, 1024),
        "b_bias_f": f(inputs["b_bias_f"]).reshape(1, 16),
        "w_branch_a": f(inputs["w_branch_a"])[0],
        "w_branch_b": f(inputs["w_branch_b"])[0],
        "w_out": f(inputs["w_out"])[0],
        "ple_norm_g": f(inputs["ple_norm_g"]).reshape(1, D),
        "w_ple_gate": f(inputs["w_ple_gate"])[0],
        "w_ple_proj": f(inputs["w_ple_proj"])[0],
        "final_norm_g": f(inputs["final_norm_g"]).reshape(1, D),
    }
    maps = []
    for b in range(n):
        m = dict(shared)
        m["x"] = np.ascontiguousarray(x[b])
        m["p"] = np.ascontiguousarray(p[b])
        maps.append(m)
    return maps


def kernel(**inputs):
    nc = _get_nc(False)
    in_maps = make_in_maps(inputs, 8)
    res = run_bass_kernel_spmd(nc, in_maps, core_ids=list(range(8)))
    out = np.stack([np.asarray(r["y"], dtype=np.float32) for r in res.results], axis=0)
    return out
```

```python
import numpy as np
from contextlib import ExitStack
import concourse.bass as bass
import concourse.mybir as mybir
from concourse.bass_utils import run_bass_kernel_spmd
from concourse.alu_op_type import AluOpType as ALU

F32 = mybir.dt.float32
BF16 = mybir.dt.bfloat16
AF = mybir.ActivationFunctionType

S_LEN = 2048
D = 1024
NT = 16
EPS = 1e-6
O_AQ, O_AK, O_AV, O_AI, O_AF, O_AO, O_AZ = 0, 512, 1024, 2048, 2052, 2056, 3080
O_BQ, O_BK, O_BV, O_BF, O_BZ, O_GA, O_GB = 4104, 5128, 6152, 7176, 7192, 8216, 9240
N_IN = 10264


ATTACH = True
PRUNE = True
MAXATT = 1
STOP_AFTER = None


class Buf:
    __slots__ = ("name", "w", "r")

    def __init__(self, name):
        self.name = name
        self.w = None
        self.r = {}


class Eng:
    def __init__(self, name):
        self.name = name
        self.q = []
        self.count = 0
        self.sem = None
        self.waited = {}


class Sched:
    def __init__(self):
        self.engs = {n: Eng(n) for n in ["pe", "act", "dve", "pool", "sp"]}
        self.dma_sems = {}
        self.ring = []
        self.ring_i = 0
        self.last_dma = {}
        self.muted = False

    def _wait(self, E, r, w, extra=()):
        deps = []
        for b in r:
            if b.w is not None:
                deps.append(b.w)
        for b in w:
            if b.w is not None:
                deps.append(b.w)
            deps.extend(b.r.values())
        deps.extend(extra)
        for (sem, val, key) in deps:
            if E.waited.get(key, 0) < val:
                E.waited[key] = val
                E.q.append(("wait", sem, val))

    def _commit(self, ticket, r, w):
        for b in r:
            old = b.r.get(ticket[2])
            if old is None or old[1] < ticket[1]:
                b.r[ticket[2]] = ticket
        for b in w:
            b.w = ticket
            b.r = {}

    def op(self, eng, fn, r=(), w=(), extra=()):
        if self.muted:
            return None
        E = self.engs[eng]
        self._wait(E, r, w, extra)
        E.count += 1
        E.q.append(("op", fn, E.sem, 1))
        t = (E.sem, E.count, eng)
        self._commit(t, r, w)
        return t

    def group(self, eng, fns, r=(), w=()):
        if self.muted:
            return None
        E = self.engs[eng]
        self._wait(E, r, w)
        for fn in fns[:-1]:
            E.q.append(("op", fn, None, 0))
        E.count += 1
        E.q.append(("op", fns[-1], E.sem, 1))
        t = (E.sem, E.count, eng)
        self._commit(t, r, w)
        return t

    def dma(self, eng, fn, stream, r=(), w=(), prefetch=False):
        if self.muted:
            return None
        E = self.engs[eng]
        self._wait(E, r, w)
        i = self.ring_i % len(self.ring)
        self.ring_i += 1
        s = self.ring[i]
        key = "dmaring_%d" % i
        if s[1] > 0 and E.waited.get(key, 0) < s[1]:
            E.waited[key] = s[1]
            E.q.append(("wait", s[0], s[1]))
        s[1] += 16
        s[2] = prefetch
        E.q.append(("op", fn, s[0], 16))
        t = (s[0], s[1], key)
        self._commit(t, r, w)
        self.last_dma[stream] = t
        return t

    def barrier(self):
        if self.muted:
            return
        tickets = []
        for n, E in self.engs.items():
            if E.count > 0:
                tickets.append((E.sem, E.count, n))
        for i, s in enumerate(self.ring):
            if s[1] > 0 and not s[2]:
                tickets.append((s[0], s[1], "dmaring_%d" % i))
        for n, E in self.engs.items():
            for (sem, val, key) in tickets:
                if key == n:
                    continue
                if E.waited.get(key, 0) < val:
                    E.waited[key] = val
                    E.q.append(("wait", sem, val))

    def finalize(self):
        needed = {n: set() for n in self.engs}
        sem2eng = {id(E.sem): n for n, E in self.engs.items()}
        for E in self.engs.values():
            for item in E.q:
                if item[0] == "wait" and id(item[1]) in sem2eng:
                    needed[sem2eng[id(item[1])]].add(item[2])
        self.remap = {}
        for n, E in self.engs.items():
            keep = sorted(needed[n])
            self.remap[n] = {old: i + 1 for i, old in enumerate(keep)}
        for n, E in self.engs.items():
            cnt = 0
            newq = []
            for item in E.q:
                if item[0] == "wait":
                    if id(item[1]) in sem2eng:
                        src = sem2eng[id(item[1])]
                        newq.append(("wait", item[1], self.remap[src][item[2]]))
                    else:
                        newq.append(item)
                else:
                    if item[2] is not None and id(item[2]) in sem2eng:
                        cnt += 1
                        if cnt in self.remap[n]:
                            newq.append(item)
                        else:
                            newq.append(("op", item[1], None, 0))
                    else:
                        newq.append(item)
            E.q = newq

    def replay(self, eng, e):
        pending = []
        for item in self.engs[eng].q:
            if item[0] == "wait":
                pending.append(item)
            else:
                if not ATTACH:
                    for w_ in pending:
                        e.wait_ge(w_[1], w_[2])
                    pending = []
                else:
                    for w_ in pending[:-MAXATT] if len(pending) > MAXATT else []:
                        e.wait_ge(w_[1], w_[2])
                    pending = pending[-MAXATT:]
                ins = item[1](e)
                for w_ in pending:
                    ins._wait_ge(w_[1], w_[2])
                pending = []
                if item[2] is not None:
                    ins.then_inc(item[2], item[3])


class Rot:
    def __init__(self, items):
        self.items = items
        self.i = 0

    def next(self):
        it = self.items[self.i % len(self.items)]
        self.i += 1
        return it


def build_nc(debug=False):
    nc = bass.Bass("TRN2", target_bir_lowering=False)

    def dram(name, shape, kind="ExternalInput"):
        return nc.dram_tensor(name, shape, F32, kind=kind).ap()

    x_d = dram("x", [S_LEN, D])
    p_d = dram("p", [S_LEN, 256])
    g_attn_d = dram("attn_norm_g", [1, D])
    win_d = dram("w_in", [D, N_IN])
    convw_d = dram("conv_w", [4, 1024])
    convb_d = dram("conv_b", [1, 1024])
    abi_d = dram("a_bias_i", [1, 4])
    abf_d = dram("a_bias_f", [1, 4])
    ghn_d = dram("a_head_norm_g", [1, 1024])
    bbf_d = dram("b_bias_f", [1, 16])
    wa_d = dram("w_branch_a", [D, D])
    wb_d = dram("w_branch_b", [D, D])
    wo_d = dram("w_out", [D, D])
    gple_d = dram("ple_norm_g", [1, D])
    wpg_d = dram("w_ple_gate", [D, D])
    wpp_d = dram("w_ple_proj", [256, D])
    gfin_d = dram("final_norm_g", [1, D])
    y_d = dram("y", [S_LEN, D], kind="ExternalOutput")
    dbg = {}
    dbg_tickets = []
    if debug:
        dbg["hT"] = nc.dram_tensor("dbg_hT", [128, 8, S_LEN], BF16, kind="ExternalOutput").ap()
        dbg["haGT"] = nc.dram_tensor("dbg_haGT", [128, 8, S_LEN], BF16, kind="ExternalOutput").ap()
        dbg["hbGT"] = nc.dram_tensor("dbg_hbGT", [128, 8, S_LEN], BF16, kind="ExternalOutput").ap()
        dbg["mT"] = nc.dram_tensor("dbg_mT", [128, 8, S_LEN], BF16, kind="ExternalOutput").ap()
        dbg["gates"] = nc.dram_tensor("dbg_gates", [128, 16, 24], F32, kind="ExternalOutput").ap()
        dbg["cum"] = nc.dram_tensor("dbg_cum", [128, 16, 16], F32, kind="ExternalOutput").ap()
        dbg["x1"] = nc.dram_tensor("dbg_x1", [128, 16, D], F32, kind="ExternalOutput").ap()

    win_v = win_d.rearrange("(kc p) n -> p kc n", p=128)

    S = Sched()
    with ExitStack() as top:
        for n in S.engs:
            S.engs[n].sem = top.enter_context(nc.semaphore("s_" + n))
        for i in range(20):
            S.ring.append([top.enter_context(nc.semaphore("dr%d" % i)), 0, False])

        def sbuf(st, name, shape, dt):
            return st.enter_context(nc.sbuf_tensor(name, shape, dt))

        P = [top.enter_context(nc.psum_tensor("P%d" % i, [128, 512], F32)) for i in range(6)]
        PB_ = [Buf("P%d" % i) for i in range(6)]
        T = [top.enter_context(nc.psum_tensor("T%d" % i, [128, 1024], BF16)) for i in range(2)]
        TB_ = [Buf("T%d" % i) for i in range(2)]
        P.append(T[0][:, :].bitcast(F32))
        PB_.append(TB_[0])
        P.append(T[1][:, :].bitcast(F32))
        PB_.append(TB_[1])

        def act(out, in_, func, r, w, **kw):
            return S.op("act", lambda e: e.activation(out=out, in_=in_, func=func, **kw), r, w)

        def tt(out, in0, in1, op, r, w, eng="dve"):
            return S.op(eng, lambda e: e.tensor_tensor(out=out, in0=in0, in1=in1, op=op), r, w)

        def ts(out, in0, s1, s2, op0, op1, r, w, eng="dve"):
            if op1 is None:
                return S.op(eng, lambda e: e.tensor_scalar(out=out, in0=in0, scalar1=s1, scalar2=None, op0=op0), r, w)
            return S.op(eng, lambda e: e.tensor_scalar(out=out, in0=in0, scalar1=s1, scalar2=s2, op0=op0, op1=op1), r, w)

        def stt(out, in0, scalar, in1, op0, op1, r, w):
            return S.op("dve", lambda e: e.scalar_tensor_tensor(out=out, in0=in0, scalar=scalar, in1=in1, op0=op0, op1=op1), r, w)

        def cp(out, in_, r, w, eng="dve"):
            return S.op(eng, lambda e: e.tensor_copy(out=out, in_=in_), r, w)

        def recip(out, in_, r, w):
            return S.op("dve", lambda e: e.reciprocal(out=out, in_=in_), r, w)

        def mm(out, lhsT, rhs, start, stop):
            return lambda e: e.matmul(out, lhsT=lhsT, rhs=rhs, start=start, stop=stop)

        def tr(out, in_, ident):
            return lambda e: e.transpose(out=out, in_=in_, identity=ident)

        def dma(eng, out, in_, stream, r, w, slow=False, prefetch=False):
            if slow:
                return S.dma(eng, lambda e: e.dma_start(out=out, in_=in_, allow_slow_non_contiguous=True), stream, r, w, prefetch)
            return S.dma(eng, lambda e: e.dma_start(out=out, in_=in_), stream, r, w, prefetch)

        def acc_group(ps_ap, pairs, r, w):
            n = len(pairs)
            fns = [mm(ps_ap, l, rr, i == 0, i == n - 1) for i, (l, rr) in enumerate(pairs)]
            return S.group("pe", fns, r, w)

        ident = sbuf(top, "ident", [128, 128], BF16)
        maskT = sbuf(top, "maskT", [128, 128], BF16)
        ones_bf = sbuf(top, "ones_bf", [128, 128], BF16)
        negmask = sbuf(top, "negmask", [128, 128], BF16)
        CONST = Buf("const")
        S.op("pool", lambda e: e.memset(ones_bf[:], 1.0), w=[CONST])
        S.op("pool", lambda e: e.memset(ident[:], 0.0), w=[CONST])
        S.op("pool", lambda e: e.affine_select(out=ident[:], in_=ident[:], pattern=[[-1, 128]], compare_op=ALU.not_equal,
                                               fill=1.0, base=0, channel_multiplier=1), w=[CONST])
        S.op("pool", lambda e: e.memset(negmask[:], 0.0), w=[CONST])
        S.op("pool", lambda e: e.affine_select(out=negmask[:], in_=negmask[:], pattern=[[1, 128]], compare_op=ALU.is_ge,
                                               fill=-240000.0, base=0, channel_multiplier=-1), w=[CONST])
        S.op("pool", lambda e: e.affine_select(out=maskT[:], in_=ones_bf[:], pattern=[[1, 128]], compare_op=ALU.is_ge,
                                               fill=0.0, base=0, channel_multiplier=-1), w=[CONST])

        m_eq = sbuf(top, "m_eq", [128, 16, 4], F32)
        m_ek = sbuf(top, "m_ek", [128, 16, 4], F32)
        m_eke = sbuf(top, "m_eke", [128, 16, 4], F32)
        m_eg = sbuf(top, "m_eg", [128, 16, 4], F32)
        f_cum = sbuf(top, "f_cum", [128, 16, 16], F32)
        f_carry = sbuf(top, "f_carry", [128, 17, 16], F32)
        SCAL = Buf("scal")

        mergedT = sbuf(top, "mergedT", [128, 8, S_LEN], BF16)
        MT = [Buf("mT%d" % i) for i in range(4)]

        with ExitStack() as sAE:
            hT = sbuf(sAE, "hT", [128, 8, S_LEN], BF16)
            HT = [Buf("hT%d" % t) for t in range(NT)]
            haGT = sbuf(sAE, "haGT", [128, 8, S_LEN], BF16)
            HA = Buf("haGT")

            wtA = sbuf(sAE, "wbuf16", [128, 8, 1024], BF16)
            wts = [wtA[:, :, :], mergedT[:, 4:8, :].rearrange("p a (b c) -> p (a b) c", c=1024)]
            WT = [Buf("wtC0"), Buf("wtC1")]
            WTQK = [Buf("wtCqk0"), Buf("wtCqk1")]

            def load_wC(h, after=(), part="all"):
                wt = wts[h % 2]
                WB_ = WT[h % 2]
                WQ_ = WTQK[h % 2]
                if part in ("all", "qk"):
                    dma("pool", wt[:, :, 0:128], win_v[:, :, O_AQ + 128 * h:O_AQ + 128 * (h + 1)], "w", list(after), [WQ_], prefetch=True)
                    dma("pool", wt[:, :, 128:256], win_v[:, :, O_AK + 128 * h:O_AK + 128 * (h + 1)], "w", list(after), [WQ_], prefetch=True)
                if part in ("all", "voz"):
                    dma("pool", wt[:, :, 256:512], win_v[:, :, O_AV + 256 * h:O_AV + 256 * (h + 1)], "w", list(after), [WB_], prefetch=True)
                    dma("pool", wt[:, :, 512:768], win_v[:, :, O_AO + 256 * h:O_AO + 256 * (h + 1)], "w", list(after), [WB_], prefetch=True)
                    dma("pool", wt[:, :, 768:1024], win_v[:, :, O_AZ + 256 * h:O_AZ + 256 * (h + 1)], "w", list(after), [WB_], prefetch=True)

            load_wC(0, part="qk")

            convw = sbuf(sAE, "convw", [128, 8, 4], F32)
            convb = sbuf(sAE, "convb", [128, 8], F32)
            CW = Buf("convw")
            wg = sbuf(sAE, "wg", [128, 8, 24], BF16)
            WG = Buf("wg")
            bias_g = sbuf(sAE, "bias_g", [128, 24], F32)
            BG = Buf("bias_g")
            dma("pool", wg[:, :, 0:8], win_v[:, :, O_AI:O_AI + 8], "w", [], [WG], slow=True)
            dma("pool", wg[:, :, 8:24], win_v[:, :, O_BF:O_BF + 16], "w", [], [WG], slow=True)
            dma("sp", bias_g[:, 0:4], abi_d.partition_broadcast(128), "c", [], [BG])
            dma("sp", bias_g[:, 4:8], abf_d.partition_broadcast(128), "c", [], [BG])
            dma("sp", bias_g[:, 8:24], bbf_d.partition_broadcast(128), "c", [], [BG])

            with ExitStack() as sA:
                xs = sbuf(sA, "xs", [128, NT, D], F32)
                XS = [Buf("xs%d" % t) for t in range(NT)]
                gbc = sbuf(sA, "gbc", [128, D], F32)
                GBC = Buf("gbc")
                ssq = sbuf(sA, "ssq", [128, NT], F32)
                rstd = sbuf(sA, "rstd", [128, NT], F32)
                SSQ = Buf("ssq")
                junk = sbuf(sA, "junk", [128, D], BF16)
                JUNK = Buf("junk")
                hb = [sbuf(sA, "hb%d" % i, [128, D], BF16) for i in range(2)]
                HB = [Buf("hb%d" % i) for i in range(2)]
                dma("sp", gbc[:], g_attn_d.partition_broadcast(128), "c", [], [GBC])
                for t in range(NT):
                    dma("sp", xs[:, t, :], x_d[t * 128:(t + 1) * 128, :], "x", [], [XS[t]])
                for k in range(4):
                    dma("sp", convw[:, :, k], convw_d[k:k + 1, :].rearrange("o (j p) -> p (o j)", p=128), "c", [], [CW], slow=True, prefetch=True)
                dma("sp", convb[:], convb_d.rearrange("o (j p) -> p (o j)", p=128), "c", [], [CW], slow=True, prefetch=True)
                SSQt = [Buf("ssq%d" % t) for t in range(NT)]
                S.op("dve", lambda e: e.memset(ssq[:], 0.0), w=SSQt)
                load_wC(0, [XS[15]], part="voz")
                def a_stats(t):
                    act(junk[:], xs[:, t, :], AF.Square, [XS[t]], [JUNK, SSQt[t]], accum_out=ssq[:, t:t + 1])
                    act(rstd[:, t:t + 1], ssq[:, t:t + 1], AF.Sqrt, [], [SSQt[t]], scale=1.0 / D, bias=EPS)
                    recip(rstd[:, t:t + 1], rstd[:, t:t + 1], [], [SSQt[t]])

                def a_norm(t):
                    b = t % 2
                    stt(hb[b][:], xs[:, t, :], rstd[:, t:t + 1], gbc[:], ALU.mult, ALU.mult, [XS[t], SSQt[t], GBC], [HB[b]])
                    fns = [tr(T[b][:, kc * 128:(kc + 1) * 128], hb[b][:, kc * 128:(kc + 1) * 128], ident[:]) for kc in range(8)]
                    S.group("pe", fns, [HB[b], CONST], [TB_[b]])

                def a_evac(t):
                    b = t % 2
                    cp(hT[:, :, t * 128:(t + 1) * 128], T[b][:, :].rearrange("p (k n) -> p k n", k=8), [TB_[b]], [HT[t]])

                a_stats(0)
                for t in range(NT + 1):
                    if t + 1 < NT:
                        a_stats(t + 1)
                    if t < NT:
                        a_norm(t)
                    if t >= 1:
                        a_evac(t - 1)
                load_wC(1, [XS[15]])
            S.barrier()
            if debug:
                dbg_tickets.append(dma("sp", dbg["hT"], hT[:], "d", HT, []))

            if STOP_AFTER == 'A':
                S.muted = True
            with ExitStack() as sC:
                sB = sC
                gpre = sbuf(sB, "gpre", [128, 16, 24], F32)
                GP = Buf("gpre")
                lf = sbuf(sB, "lf", [128, 16, 20], F32)
                lr = sbuf(sB, "lr", [128, 16, 20], F32)
                LF = Buf("lf")
                parts = [sbuf(sB, "lfp%d" % i, [128, 320], BF16) for i in range(3)]
                PR = Buf("parts")
                bcs = sbuf(sB, "bcs", [128, 16, 20], F32)
                tot = sbuf(sB, "tot", [128, 16, 20], F32)
                BC = Buf("bcs")
                tmpa = sbuf(sB, "tmpa", [128, 16, 4], F32)
                tmpb = sbuf(sB, "tmpb", [128, 16, 4], F32)
                TM = Buf("tmpab")
                def phaseB_part1():
                    for t in range(NT):
                        acc_group(P[0][:, t * 24:(t + 1) * 24],
                                  [(hT[:, kc, t * 128:(t + 1) * 128], wg[:, kc, :]) for kc in range(8)],
                                  [HT[t], WG], [PB_[0]])
                    for t in range(NT):
                        tt(gpre[:, t, :], P[0][:, t * 24:(t + 1) * 24], bias_g[:], ALU.add, [PB_[0], BG], [GP])
                    if debug:
                        dbg_tickets.append(dma("sp", dbg["gates"], gpre[:], "d", [GP], []))
                    act(lf[:], gpre[:, :, 4:24], AF.Exp, [GP], [LF], scale=-1.0)
                    act(lf[:], lf[:], AF.Ln, [], [LF], bias=1.0)
                    ts(lf[:], lf[:], -1.0, None, ALU.mult, None, [], [LF])
                    lf2 = lf[:].rearrange("p t g -> p (t g)")
                    lr2 = lr[:].rearrange("p t g -> p (t g)")
                    cp(parts[0][:], lf2, [LF], [PR])
                    tt(lr2, lf2, parts[0][:], ALU.subtract, [PR], [LF])
                    cp(parts[1][:], lr2, [LF], [PR])
                    tt(lr2, lr2, parts[1][:], ALU.subtract, [PR], [LF])
                    cp(parts[2][:], lr2, [LF], [PR])

                def phaseB_part2():
                    acc_group(P[1][:, 0:320], [(maskT[:], parts[i][:]) for i in range(3)], [PR, CONST], [PB_[1]])
                    acc_group(P[2][:, 0:320], [(ones_bf[:], parts[i][:]) for i in range(3)], [PR, CONST], [PB_[2]])
                    act(bcs[:].rearrange("p t g -> p (t g)"), P[1][:, 0:320], AF.Copy, [PB_[1]], [BC])
                    act(tot[:].rearrange("p t g -> p (t g)"), P[2][:, 0:320], AF.Copy, [PB_[2]], [BC])
                    cq = float(128 ** -0.5)
                    tt(tmpa[:], gpre[:, :, 0:4], bcs[:, :, 0:4], ALU.subtract, [GP, BC], [TM])
                    tt(tmpb[:], tmpa[:], tot[:, :, 0:4], ALU.add, [BC], [TM])
                    act(m_ek[:], tmpa[:], AF.Exp, [TM], [SCAL])
                    act(m_eke[:], tmpb[:], AF.Exp, [TM], [SCAL])
                    act(m_eq[:], bcs[:, :, 0:4], AF.Exp, [BC], [SCAL])
                    act(m_eg[:], tot[:, :, 0:4], AF.Exp, [BC], [SCAL])
                    ts(m_ek[:], m_ek[:], cq, None, ALU.mult, None, [], [SCAL])
                    ts(m_eke[:], m_eke[:], cq, None, ALU.mult, None, [], [SCAL])
                    S.op("dve", lambda e: e.memset(f_carry[:, 0, :], 0.0), w=[SCAL])
                    for j in range(NT):
                        tt(f_carry[:, j + 1, :], f_carry[:, j, :], tot[:, j, 4:20], ALU.add, [BC], [SCAL])
                    tt(f_cum[:], bcs[:, :, 4:20], f_carry[:, 0:16, :], ALU.add, [BC], [SCAL])
                    if debug:
                        dbg_tickets.append(dma("sp", dbg["cum"], f_cum[:], "d", [SCAL], []))


                pre_bf = [sbuf(sC, "pre_bf%d" % i, [128, 2052], BF16) for i in range(2)]
                PRE = [[Buf("pre%d_%d" % (i, tb)) for tb in range(4)] for i in range(2)]
                diagw = sbuf(sC, "diagw", [128, 8, 4, 128], BF16)
                DGW = Buf("diagw")
                hraw = mergedT[:, 0:4, :].rearrange("p a b -> p (a b)").bitcast(F32).rearrange("p (t f) -> p t f", t=16)
                HR = Buf("hraw")
                qT = sbuf(sC, "qT", [128, S_LEN], BF16)
                kT = sbuf(sC, "kT", [128, S_LEN], BF16)
                QK = [Buf("qTb"), Buf("kTb")]
                v_aug = sbuf(sC, "v_aug", [128, NT, 258], BF16)
                VA = [Buf("v_aug%d" % t) for t in range(NT)]
                Gh = sbuf(sC, "Gh", [128, NT, 256], BF16)
                GH = [Buf("Gh%d" % t) for t in range(NT)]
                ks_end = sbuf(sC, "ks_end", [128, NT, 128], BF16)
                KE = [Buf("ks_end%d" % t) for t in range(NT)]
                Cbf = sbuf(sC, "Cbf", [128, NT, 258], BF16)
                CB = [Buf("Cbf%d" % t) for t in range(NT)]
                Cf = [sbuf(sC, "Cf%d" % i, [128, 257], F32) for i in range(2)]
                CF = [Buf("Cf%d" % i) for i in range(2)]
                ghn = sbuf(sC, "ghn", [128, 1024], F32)
                GHN = Buf("ghn")
                to_t = [sbuf(sC, "to%d" % i, [128, 256], F32) for i in range(2)]
                sz_t = [sbuf(sC, "sz%d" % i, [128, 256], F32) for i in range(2)]
                TO = [Buf("to%d" % i) for i in range(2)]
                SZB = [Buf("sz%d" % i) for i in range(2)]
                sT = [sbuf(sC, "sT%d" % i, [128, 128], BF16) for i in range(2)]
                STB = [Buf("sT%d" % i) for i in range(2)]
                hag = [sbuf(sC, "hag%d" % i, [128, 256], BF16) for i in range(2)]
                HG = [Buf("hag%d" % i) for i in range(2)]
                junkc = sbuf(sC, "junkc", [128, 256], BF16)
                JC = Buf("junkc")
                ssr = sbuf(sC, "ssr", [128, NT], F32)
                den = sbuf(sC, "den", [128, NT], F32)
                fac = sbuf(sC, "fac", [128, NT], F32)
                stot = sbuf(sC, "stot", [128, NT], F32)
                tmps = sbuf(sC, "tmps", [128, NT], F32)
                SS = Buf("ssr")
                ST_ = Buf("stot")

                phaseB_part1()
                dma("sp", ghn[:], ghn_d.partition_broadcast(128), "c", [], [GHN])
                for i in range(2):
                    S.op("dve", lambda e, i=i: e.memset(pre_bf[i][:, 0:3], 0.0), w=[PRE[i][0]])
                for j in range(8):
                    for k in range(4):
                        wl = [DGW] if (j, k) in ((0, 0), (7, 3)) else []
                        ts(diagw[:, j, k, :], ident[:], convw[:, j, k:k + 1], None, ALU.mult, None, [CW, CONST], wl)
                ts(ghn[:], ghn[:], 0.5, None, ALU.mult, None, [], [GHN])
                S.op("pool", lambda e: e.memset(v_aug[:, :, 256:258], 1.0), w=VA)
                projP = Rot([0, 1])
                trR = Rot([0, 1])

                def ktr_tile(h, t):
                    b = trR.next()
                    S.group("pe", [tr(T[b][:, 0:128], kT[:, t * 128:(t + 1) * 128], ident[:])], [QK[1], CONST], [TB_[b]])
                    ts(ks_end[:, t, :], T[b][:, 0:128], m_eke[:, t, h:h + 1], None, ALU.mult, None, [TB_[b], SCAL], [KE[t]])

                def stage_qk(h, with_ktr=False):
                    wt = wts[h % 2]
                    WB_ = WTQK[h % 2]

                    def proj(wi, tb):
                        coff = 128 * wi
                        pi = projP.next()
                        acc_group(P[pi][:, :], [(wt[:, kc, coff:coff + 128], hT[:, kc, tb * 512:(tb + 1) * 512]) for kc in range(8)],
                                  [WB_] + HT[tb * 4:tb * 4 + 4], [PB_[pi]])
                        act(pre_bf[wi][:, 3 + tb * 512:3 + (tb + 1) * 512], P[pi][:, :], AF.Copy, [PB_[pi]], [PRE[wi][tb]])

                    def conv(wi, tb):
                        dst = (qT, kT)[wi]
                        j = wi * 4 + h
                        pi = projP.next()
                        rds = [DGW, PRE[wi][tb]] + ([PRE[wi][tb - 1]] if tb > 0 else [])
                        acc_group(P[pi][:, :], [(diagw[:, j, k, :], pre_bf[wi][:, tb * 512 + k:tb * 512 + k + 512]) for k in range(4)],
                                  rds, [PB_[pi]])
                        act(dst[:, tb * 512:(tb + 1) * 512], P[pi][:, :], AF.Silu, [PB_[pi], CW], [QK[wi]], bias=convb[:, j:j + 1])

                    for tb in range(4):
                        proj(1, tb)
                    for tb in range(4):
                        conv(1, tb)
                    for tb in range(4):
                        proj(0, tb)
                        if with_ktr:
                            for t in range(tb * 4, tb * 4 + 4):
                                ktr_tile(h, t)
                    for tb in range(4):
                        conv(0, tb)

                def stage_ktr(h):
                    for t in range(NT):
                        b = trR.next()
                        S.group("pe", [tr(T[b][:, 0:128], kT[:, t * 128:(t + 1) * 128], ident[:])], [QK[1], CONST], [TB_[b]])
                        ts(ks_end[:, t, :], T[b][:, 0:128], m_eke[:, t, h:h + 1], None, ALU.mult, None, [TB_[b], SCAL], [KE[t]])

                def emit_proj(h, t):
                    wt = wts[h % 2]
                    WB_ = WT[h % 2]
                    pv = projP.next()
                    acc_group(P[pv][:, 0:256], [(hT[:, kc, t * 128:(t + 1) * 128], wt[:, kc, 256:512]) for kc in range(8)],
                              [WB_, HT[t]], [PB_[pv]])
                    po = projP.next()
                    acc_group(P[po][:, 0:512], [(hT[:, kc, t * 128:(t + 1) * 128], wt[:, kc, 512:1024]) for kc in range(8)],
                              [WB_, HT[t]], [PB_[po]])

                    def ev():
                        act(v_aug[:, t, 0:256], P[pv][:, 0:256], AF.Copy, [PB_[pv]], [VA[t]])
                        b = t % 2
                        act(to_t[b][:], P[po][:, 0:256], AF.Tanh, [PB_[po]], [TO[b]], scale=0.5)
                        act(sz_t[b][:], P[po][:, 256:512], AF.Silu, [PB_[po]], [SZB[b]])
                        stt(to_t[b][:], to_t[b][:], 1.0, ghn[:, h * 256:(h + 1) * 256], ALU.add, ALU.mult, [GHN], [TO[b]])
                        tt(Gh[:, t, :], to_t[b][:], sz_t[b][:], ALU.mult, [TO[b], SZB[b]], [GH[t]], eng="pool")
                    return ev

                def stage_loop(h, pre=None):
                    S.op("dve", lambda e: e.memset(ssr[:], 0.0), w=[SS])
                    emit_proj(h, 0)()
                    done1 = False
                    if pre is not None:
                        emit_proj(h, 1)()
                        done1 = True
                        pre()
                    for t in range(NT):
                        b = t % 2
                        if t < NT - 1:
                            acc_group(P[4][:, 0:257], [(ks_end[:, t, :], v_aug[:, t, 0:257])], [KE[t], VA[t]], [PB_[4]])
                        acc_group(P[2][:, 0:128], [(kT[:, t * 128:(t + 1) * 128], qT[:, t * 128:(t + 1) * 128])], [QK[0], QK[1]], [PB_[2]])
                        ev = emit_proj(h, t + 1) if (t + 1 < NT and not (t == 0 and done1)) else None
                        if t < NT - 1:
                            cb = t % 2
                            if t == 0:
                                cp(Cf[cb][:], P[4][:, 0:257], [PB_[4]], [CF[cb]])
                            else:
                                stt(Cf[cb][:], Cf[1 - cb][:], m_eg[:, t, h:h + 1], P[4][:, 0:257], ALU.mult, ALU.add,
                                    [PB_[4], CF[1 - cb], SCAL], [CF[cb]])
                            act(Cbf[:, t, 0:257], Cf[cb][:], AF.Copy, [CF[cb]], [CB[t]])
                        stt(sT[b][:], P[2][:, 0:128], m_ek[:, t, h:h + 1], maskT[:], ALU.mult, ALU.mult, [PB_[2], SCAL, CONST], [STB[b]])
                        if ev is not None:
                            ev()
                        pairs = []
                        rds = [STB[b], VA[t], QK[0]]
                        if t > 0:
                            pairs.append((qT[:, t * 128:(t + 1) * 128], Cbf[:, t - 1, 0:257]))
                            rds.append(CB[t - 1])
                        pairs.append((sT[b][:], v_aug[:, t, 0:257]))
                        acc_group(P[3][:, 0:257], pairs, rds, [PB_[3]])
                        act(junkc[:], P[3][:, 0:256], AF.Square, [PB_[3]], [JC, SS], accum_out=ssr[:, t:t + 1])
                        cp(den[:, t:t + 1], P[3][:, 256:257], [PB_[3]], [SS])
                        cp(hraw[:, t, :], P[3][:, 0:256], [PB_[3]], [HR])

                def stage_final(h):
                    tt(tmps[:], den[:], m_eq[:, :, h], ALU.mult, [SCAL, SS], [ST_])
                    stt(tmps[:], tmps[:], -1.0, tmps[:], ALU.mult, ALU.max, [], [ST_])
                    ts(tmps[:], tmps[:], 1.0, None, ALU.max, None, [], [ST_])
                    S.op("dve", lambda e: e.reciprocal(out=tmps[:], in_=tmps[:]), w=[ST_])
                    tt(fac[:], tmps[:], m_eq[:, :, h], ALU.mult, [SCAL], [ST_])
                    tt(tmps[:], ssr[:], fac[:], ALU.mult, [SS], [ST_])
                    tt(tmps[:], tmps[:], fac[:], ALU.mult, [], [ST_])
                    act(tmps[:], tmps[:], AF.Sqrt, [], [ST_], scale=1.0 / 256, bias=EPS)
                    S.op("dve", lambda e: e.reciprocal(out=tmps[:], in_=tmps[:]), w=[ST_])
                    tt(stot[:], tmps[:], fac[:], ALU.mult, [], [ST_])
                    for t in range(NT):
                        b = t % 2
                        stt(hag[b][:], hraw[:, t, :], stot[:, t:t + 1], Gh[:, t, :], ALU.mult, ALU.mult, [HR, ST_, GH[t]], [HG[b]])
                        tb_i = trR.next()
                        S.group("pe", [tr(T[tb_i][:, 0:128], hag[b][:, 0:128], ident[:]), tr(T[tb_i][:, 128:256], hag[b][:, 128:256], ident[:])],
                                [HG[b], CONST], [TB_[tb_i]])
                        act(haGT[:, 2 * h:2 * h + 2, t * 128:(t + 1) * 128], T[tb_i][:, 0:256].rearrange("p (k n) -> p k n", k=2),
                            AF.Copy, [TB_[tb_i]], [HA])

                stage_qk(0)
                phaseB_part2()
                stage_loop(0, pre=lambda: stage_ktr(0))
                for h in range(1, 4):
                    if h + 1 < 4:
                        load_wC(h + 1)
                    if h == 3:
                        for i, off in enumerate((O_BQ, O_BK, O_BV, O_BZ)):
                            dma("pool", wtA[:, :, i * 128:(i + 1) * 128], win_v[:, :, off:off + 128], "w", [], [WT[0], WTQK[0]], prefetch=True)
                    stage_qk(h, with_ktr=True)
                    stage_final(h - 1)
                    stage_loop(h)
                stage_final(3)
            S.barrier()
            if debug:
                dbg_tickets.append(dma("sp", dbg["haGT"], haGT[:], "d", [HA], []))

            hbGT = sbuf(sAE, "hbGT", [128, 8, S_LEN], BF16)
            HBG = Buf("hbGT")

            with ExitStack() as sD:
                wtD = [sbuf(sD, "wtD%d" % i, [128, 8, 512], BF16) for i in range(2)]
                WTD = [Buf("wtD%d" % i) for i in range(2)]
                qhA = [sbuf(sD, "qh%d" % i, [128, S_LEN], BF16) for i in range(2)]
                khA = [sbuf(sD, "kh%d" % i, [128, S_LEN], BF16) for i in range(2)]
                VpA = sbuf(sD, "Vp", [128, NT, 192], BF16)
                SZA = sbuf(sD, "SZ", [128, S_LEN], BF16)
                mflat = mergedT[:, 4:6, :].rearrange("p a b -> p (a b)")
                sets = [
                    dict(qh=[qhA[0][:, :], qhA[1][:, :]], kh=[khA[0][:, :], khA[1][:, :]], Vp=VpA[:, :, :], SZ=SZA[:, :]),
                    dict(qh=[mergedT[:, 0, :], mergedT[:, 1, :]], kh=[mergedT[:, 2, :], mergedT[:, 3, :]],
                         Vp=mflat[:, 0:3072].rearrange("p (t c) -> p t c", c=192), SZ=mergedT[:, 6, :]),
                ]
                for i, st_ in enumerate(sets):
                    st_["QD"] = [Buf("qh%d_%d" % (i, k)) for k in range(2)]
                    st_["KD"] = [Buf("kh%d_%d" % (i, k)) for k in range(2)]
                    st_["VP"] = Buf("Vp%d" % i)
                    st_["SZD"] = Buf("SZ%d" % i)
                tz = [sbuf(sD, "tz%d" % i, [128, 512], F32) for i in range(2)]
                TZ = [Buf("tz%d" % i) for i in range(2)]
                PT = [sbuf(sD, "PT%d" % i, [128, 512], BF16) for i in range(6)]
                PTB = [Buf("PT%d" % i) for i in range(6)]
                negcum = sbuf(sD, "negcum", [128, 16, 16], F32)
                carry8 = sbuf(sD, "carry8", [128, 16, 16], F32)
                BD = Buf("biasD")
                rd = [sbuf(sD, "rd%d" % i, [128, 512], F32) for i in range(2)]
                RD = [Buf("rd%d" % i) for i in range(2)]
                ts(negcum[:], f_cum[:], -1.0, None, ALU.mult, None, [SCAL], [BD])
                ts(carry8[:], f_carry[:, 0:16, :], 8.0, None, ALU.mult, None, [SCAL], [BD])

                def memset(eng, ap, val, w):
                    return S.op(eng, lambda e: e.memset(ap, val), w=w)

                def load_wD(pr):
                    w_ = wtD[pr % 2]
                    for i, off in enumerate((O_BQ, O_BK, O_BV, O_BZ)):
                        dma("pool", w_[:, :, i * 128:(i + 1) * 128], win_v[:, :, off + 128 * pr:off + 128 * (pr + 1)], "w", [], [WTD[pr % 2]])

                for si, st_ in enumerate(sets):
                    e0, e1 = ("dve", "pool") if si == 0 else ("pool", "dve")
                    memset("pool", st_["Vp"][:, :, 64:128], 2.0, [st_["VP"]])
                    for k in range(2):
                        memset(e0, st_["qh"][k][64:128, :], 0.0, [st_["QD"][k]])
                        memset(e1, st_["kh"][k][64:128, :], 0.0, [st_["KD"][k]])
                        memset(e1, st_["kh"][k][64:65, :], 1.0, [st_["KD"][k]])

                projP = Rot([0, 1])
                sP = Rot([2, 3, 7])
                oP = Rot([4, 5, 6])
                ptR = Rot([0, 1, 2, 3, 4, 5])
                tzR = Rot([0, 1])
                rdR = Rot([0, 1])
                norm_q = []

                def proj_units(pr):
                    st_ = sets[pr % 2]
                    w_ = wtD[pr % 2] if pr > 0 else wtA[:, :, 0:512]
                    WB_ = WTD[pr % 2] if pr > 0 else WT[0]
                    units = []

                    def u_q(tb):
                        hts = HT[tb * 4:tb * 4 + 4]
                        tsl = slice(tb * 512, (tb + 1) * 512)
                        pi = projP.next()
                        acc_group(P[pi][:, :], [(w_[:, kc, 0:128], hT[:, kc, tsl]) for kc in range(8)], [WB_] + hts, [PB_[pi]])

                        def ev():
                            cp(st_["qh"][0][0:64, tsl], P[pi][0:64, :], [PB_[pi]], [st_["QD"][0]])
                            act(st_["qh"][1][0:64, tsl], P[pi][64:128, :], AF.Copy, [PB_[pi]], [st_["QD"][1]])
                        return ev

                    def u_k(tb):
                        hts = HT[tb * 4:tb * 4 + 4]
                        tsl = slice(tb * 512, (tb + 1) * 512)
                        pi = projP.next()
                        acc_group(P[pi][:, :], [(w_[:, kc, 128:256], hT[:, kc, tsl]) for kc in range(8)], [WB_] + hts, [PB_[pi]])

                        def ev():
                            cp(st_["kh"][0][0:64, tsl], P[pi][0:64, :], [PB_[pi]], [st_["KD"][0]])
                            cp(st_["kh"][1][0:64, tsl], P[pi][64:128, :], [PB_[pi]], [st_["KD"][1]])
                        return ev

                    def u_z(tb):
                        hts = HT[tb * 4:tb * 4 + 4]
                        tsl = slice(tb * 512, (tb + 1) * 512)
                        pi = projP.next()
                        acc_group(P[pi][:, :], [(w_[:, kc, 384:512], hT[:, kc, tsl]) for kc in range(8)], [WB_] + hts, [PB_[pi]])

                        def ev():
                            b = tzR.next()
                            act(tz[b][:], P[pi][:, :], AF.Tanh, [PB_[pi]], [TZ[b]], scale=0.5)
                            stt(st_["SZ"][:, tsl], tz[b][:], 1.0, P[pi][:, :], ALU.add, ALU.mult, [TZ[b], PB_[pi]], [st_["SZD"]])
                        return ev

                    def u_shift(hh):
                        head = 2 * pr + hh

                        def ev():
                            for bq in range(NT):
                                wl = [st_["QD"][hh]] if bq in (0, NT - 1) else []
                                ts(st_["qh"][hh][64:65, bq * 128:(bq + 1) * 128], ones_bf[64:65, :], carry8[64:65, bq, head:head + 1], None,
                                   ALU.mult, None, [BD, CONST], wl)
                        return ev

                    def u_v(t):
                        pi = projP.next()
                        acc_group(P[pi][:, 0:128], [(hT[:, kc, t * 128:(t + 1) * 128], w_[:, kc, 256:384]) for kc in range(8)], [WB_, HT[t]], [PB_[pi]])

                        def ev():
                            act(st_["Vp"][:, t, :].rearrange("p (a b) -> p a b", b=64)[:, 0:3:2, :],
                                P[pi][:, 0:128].rearrange("p (a b) -> p a b", b=64), AF.Copy, [PB_[pi]], [st_["VP"]])
                        return ev

                    for tb in range(4):
                        units.append(lambda tb=tb: u_q(tb))
                        units.append(lambda tb=tb: u_k(tb))
                        units.append(lambda tb=tb: u_z(tb))
                    for hh in range(2):
                        units.append(lambda hh=hh: u_shift(hh))
                    for t in range(NT):
                        units.append(lambda t=t: u_v(t))
                    return units

                for u in proj_units(0):
                    u()()
                WAB = Buf("wab")
                for pr in range(8):
                    st_ = sets[pr % 2]
                    pending = []
                    if pr == 1:
                        dma("pool", wtA[:], wa_d.rearrange("(kc p) n -> p kc n", p=128), "w", [], [WAB, WT[0], WTQK[0]])
                    if pr + 1 < 8:
                        load_wD(pr + 1)
                        pending = proj_units(pr + 1)
                    nsteps = 60
                    step_i = 0
                    emitted = 0
                    deferred = []
                    for hh in range(2):
                        head = 2 * pr + hh
                        qh_, kh_ = st_["qh"][hh], st_["kh"][hh]
                        QD_, KD_ = st_["QD"][hh], st_["KD"][hh]
                        Vp_, VP_ = st_["Vp"], st_["VP"]
                        vcols = slice(0, 128) if hh == 0 else slice(64, 192)
                        nrows = slice(0, 64) if hh == 0 else slice(64, 128)
                        drows = slice(64, 128) if hh == 0 else slice(0, 64)
                        for g in range(4):
                            po = oP.next()
                            nj = 4 * g + 4

                            def emit_mm1(j):
                                ps_i = sP.next()
                                q0 = max(4 * g, j) * 128
                                q1 = (4 * g + 4) * 128
                                c0 = q0 - 4 * g * 128
                                if j >= 4 * g:
                                    fns = [mm(P[ps_i][:, c0:512], kh_[:, j * 128:(j + 1) * 128], qh_[:, q0:q1], True, False),
                                           mm(P[ps_i][:, c0:c0 + 128], ident[:], negmask[:], False, True)]
                                    S.group("pe", fns, [QD_, KD_, CONST], [PB_[ps_i]])
                                else:
                                    acc_group(P[ps_i][:, c0:512], [(kh_[:, j * 128:(j + 1) * 128], qh_[:, q0:q1])], [QD_, KD_], [PB_[ps_i]])
                                return ps_i

                            LOOK = 2
                            banks = [emit_mm1(jj) for jj in range(min(LOOK, nj))]
                            for j in range(nj):
                                cur = banks[j]
                                pti = ptR.next()
                                b0 = max(4 * g, j)
                                c0 = (b0 - 4 * g) * 128
                                act(PT[pti][:, c0:512], P[cur][:, c0:512], AF.Exp, [PB_[cur], BD], [PTB[pti]],
                                    scale=0.125, bias=negcum[:, j, head:head + 1])
                                for ev in deferred:
                                    ev()
                                deferred = []
                                if norm_q and j >= 1:
                                    norm_q.pop(0)()
                                if j + LOOK < nj:
                                    banks.append(emit_mm1(j + LOOK))
                                step_i += 1
                                want = (len(pending) * step_i + nsteps - 1) // nsteps if pending else 0
                                while emitted < min(want, len(pending)):
                                    deferred.append(pending[emitted]())
                                    emitted += 1
                                fn = mm(P[po][:, c0:512], Vp_[:, j, vcols], PT[pti][:, c0:512], j == 0, j == nj - 1)
                                S.group("pe", [fn], [VP_, PTB[pti]], [PB_[po]])
                            def norm_piece(qq, po=po, g=g, nrows=nrows, drows=drows, SZ_=st_["SZ"], SZD_=st_["SZD"], pr=pr):
                                b = rdR.next()
                                cs = slice(qq * 256, (qq + 1) * 256)
                                gs = slice(g * 512 + qq * 256, g * 512 + (qq + 1) * 256)
                                recip(rd[b][nrows, cs], P[po][drows, cs], [PB_[po]], [RD[b]])
                                tt(rd[b][nrows, cs], rd[b][nrows, cs], SZ_[nrows, gs], ALU.mult, [SZD_], [RD[b]])
                                tt(hbGT[nrows, pr, gs], P[po][nrows, cs], rd[b][nrows, cs], ALU.mult, [PB_[po], RD[b]], [HBG])
                            norm_piece(0)
                            norm_q.append(lambda f=norm_piece: f(1))
                    for ev in deferred:
                        ev()
                    while norm_q:
                        norm_q.pop(0)()
                    while emitted < len(pending):
                        pending[emitted]()()
                        emitted += 1
            S.barrier()
            if debug:
                dbg_tickets.append(dma("sp", dbg["hbGT"], hbGT[:], "d", [HBG], []))

            wo = sbuf(sAE, "wo", [128, 8, D], BF16)
            wpp = sbuf(sAE, "wpp", [128, 2, D], BF16)
            WO = [Buf("wo%d" % i) for i in range(2)]
            WPP = Buf("wpp")
            with ExitStack() as sE:
                wa = wtA
                wb = sbuf(sE, "wb", [128, 8, D], BF16)
                wgE = [sbuf(sE, "wgE%d" % i, [128, 8, 256], BF16) for i in range(2)]
                WGE = [Buf("wgE%d" % i) for i in range(2)]
                ta = [sbuf(sE, "ta%d" % i, [128, 512], F32) for i in range(2)]
                tb_ = [sbuf(sE, "tb%d" % i, [128, 512], F32) for i in range(2)]
                TA = [Buf("ta%d" % i) for i in range(2)]
                TBb = [Buf("tb%d" % i) for i in range(2)]
                WBc = [Buf("wb%d" % c) for c in range(8)]
                wb_v = wb_d.rearrange("(kc p) n -> p kc n", p=128)

                def load_wE(c):
                    dma("pool", wgE[c % 2][:, :, 0:128], win_v[:, :, O_GA + 128 * c:O_GA + 128 * (c + 1)], "w", [], [WGE[c % 2]])
                    dma("pool", wgE[c % 2][:, :, 128:256], win_v[:, :, O_GB + 128 * c:O_GB + 128 * (c + 1)], "w", [], [WGE[c % 2]])

                load_wE(0)
                for c in range(8):
                    dma("pool", wb[:, :, c * 128:(c + 1) * 128], wb_v[:, :, c * 128:(c + 1) * 128], "w", [], [WBc[c]])
                it = 0
                wo_v = wo_d.rearrange("(kc p) n -> p kc n", p=128)
                for c in range(8):
                    if c + 1 < 8:
                        load_wE(c + 1)
                    if c == 2:
                        for i in range(2):
                            dma("pool", wo[:, :, i * 512:(i + 1) * 512], wo_v[:, :, i * 512:(i + 1) * 512], "w", [], [WO[i]], prefetch=True)
                        dma("pool", wpp[:], wpp_d.rearrange("(kc p) n -> p kc n", p=128), "w", [], [WPP], prefetch=True)
                    wg_ = wgE[c % 2]
                    for tb in range(4):
                        hts = HT[tb * 4:tb * 4 + 4]
                        tsl = slice(tb * 512, (tb + 1) * 512)
                        b = it % 2
                        it += 1
                        pga, pgb = (0, 1) if b == 0 else (4, 5)
                        acc_group(P[pga][:, :], [(wg_[:, kc, 0:128], hT[:, kc, tsl]) for kc in range(8)], [WGE[c % 2]] + hts, [PB_[pga]])
                        acc_group(P[pgb][:, :], [(wg_[:, kc, 128:256], hT[:, kc, tsl]) for kc in range(8)], [WGE[c % 2]] + hts, [PB_[pgb]])
                        act(ta[b][:], P[pga][:, :], AF.Tanh, [PB_[pga]], [TA[b]], scale=0.5)
                        act(tb_[b][:], P[pgb][:, :], AF.Tanh, [PB_[pgb]], [TBb[b]], scale=0.5)
                        acc_group(P[2][:, :], [(wa[:, kc, c * 128:(c + 1) * 128], haGT[:, kc, tsl]) for kc in range(8)], [WAB, HA], [PB_[2]])
                        acc_group(P[3][:, :], [(wb[:, kc, c * 128:(c + 1) * 128], hbGT[:, kc, tsl]) for kc in range(8)], [WBc[c], HBG], [PB_[3]])
                        stt(ta[b][:], ta[b][:], 1.0, P[2][:, :], ALU.add, ALU.mult, [PB_[2]], [TA[b]])
                        stt(tb_[b][:], tb_[b][:], 1.0, P[3][:, :], ALU.add, ALU.mult, [PB_[3]], [TBb[b]])
                        tt(mergedT[:, c, tsl], ta[b][:], tb_[b][:], ALU.add, [TA[b], TBb[b]], [MT[tb]], eng="pool")
            S.barrier()
            if debug:
                dbg_tickets.append(dma("sp", dbg["mT"], mergedT[:], "d", MT, []))

            S.barrier()
            with ExitStack() as sF:
                wpg = wtA
                WF = Buf("wF")
                _hv = hT[:, :, :].rearrange("p a b -> p (a b)").bitcast(F32).rearrange("p (t d) -> p t d", d=D)
                _av = haGT[:, :, :].rearrange("p a b -> p (a b)").bitcast(F32).rearrange("p (t d) -> p t d", d=D)
                x1t = [_hv[:, t, :] for t in range(8)] + [_av[:, t, :] for t in range(8)]
                X1 = [Buf("x1_%d" % t) for t in range(NT)]
                xin = [hbGT[:, 4 + i, :].bitcast(F32) for i in range(2)]
                XIN = [Buf("xin%d" % i) for i in range(2)]
                pin = [sbuf(sF, "pin%d" % i, [128, 256], F32) for i in range(2)]
                PIN = [Buf("pin%d" % i) for i in range(2)]
                pbf = [sbuf(sF, "pbf%d" % i, [128, 256], BF16) for i in range(3)]
                PBF = [Buf("pbf%d" % i) for i in range(3)]
                pT = [sbuf(sF, "pT%d" % i, [128, 2, 128], BF16) for i in range(2)]
                PTF = [Buf("pTf%d" % i) for i in range(2)]
                gple = sbuf(sF, "gple", [128, D], F32)
                gfin = sbuf(sF, "gfin", [128, D], F32)
                GF = Buf("gF")
                ss1 = sbuf(sF, "ss1", [128, NT], F32)
                rs1 = sbuf(sF, "rs1", [128, NT], F32)
                ss2 = sbuf(sF, "ss2", [128, NT], F32)
                rs2 = sbuf(sF, "rs2", [128, NT], F32)
                SS1, SS2 = Buf("ss1"), Buf("ss2")
                junkf = sbuf(sF, "junkf", [128, D], BF16)
                JF = Buf("junkf")
                xn = [hbGT[:, 6, i * D:(i + 1) * D] for i in range(2)] + [hbGT[:, 7, 0:D]]
                XN = [Buf("xn%d" % i) for i in range(3)]
                xnT = [sbuf(sF, "xnT%d" % i, [128, 8, 128], BF16) for i in range(2)]
                XNT = [Buf("xnT%d" % i) for i in range(2)]
                tg = [hbGT[:, i, :].bitcast(F32) for i in range(2)]
                TG = [Buf("tg%d" % i) for i in range(2)]
                ob = [hbGT[:, 2 + i, :].bitcast(F32) for i in range(2)]
                OB = [Buf("ob%d" % i) for i in range(2)]
                WPG = Buf("wpg")
                dma("sp", gple[:], gple_d.partition_broadcast(128), "c", [], [GF])
                dma("sp", gfin[:], gfin_d.partition_broadcast(128), "c", [], [GF])
                wpg_v = wpg_d.rearrange("(kc p) n -> p kc n", p=128)
                for i in range(2):
                    dma("pool", wpg[:, :, i * 512:(i + 1) * 512], wpg_v[:, :, i * 512:(i + 1) * 512], "w", [], [WPG])
                S1 = [Buf("ss1_%d" % t) for t in range(NT)]
                S2 = [Buf("ss2_%d" % t) for t in range(NT)]
                S.op("dve", lambda e: e.memset(ss1[:], 0.0), w=S1)
                S.op("dve", lambda e: e.memset(ss2[:], 0.0), w=S2)
                P4b = [P[4][:, :].bitcast(BF16), P[5][:, :].bitcast(BF16)]
                outs = []

                def f_A1(t):
                    b = t % 2
                    for half in range(2):
                        pi = half
                        acc_group(P[pi][:, :], [(mergedT[:, kc, t * 128:(t + 1) * 128], wo[:, kc, half * 512:(half + 1) * 512]) for kc in range(8)],
                                  [MT[t // 4], WO[half]], [PB_[pi]])
                        stt(x1t[t][:, half * 512:(half + 1) * 512], P[pi][:, :], 0.5, xin[b][:, half * 512:(half + 1) * 512], ALU.mult, ALU.add,
                            [PB_[pi], XIN[b]], [X1[t]])

                def f_B1(t):
                    b = t % 2
                    b3 = t % 3
                    cp(pbf[b3][:], pin[b][:], [PIN[b]], [PBF[b3]], eng="pool")
                    stt(xn[b3][:], x1t[t][:, :], rs1[:, t:t + 1], gple[:], ALU.mult, ALU.mult, [X1[t], S1[t], GF], [XN[b3]])

                def f_B2(t):
                    b = t % 2
                    b3 = t % 3
                    fns = [tr(T[0][:, kc * 128:(kc + 1) * 128], xn[b3][:, kc * 128:(kc + 1) * 128], ident[:]) for kc in range(8)]
                    S.group("pe", fns, [XN[b3], CONST], [TB_[0]])
                    fns = [tr(T[1][:, kc * 128:(kc + 1) * 128], pbf[b3][:, kc * 128:(kc + 1) * 128], ident[:]) for kc in range(2)]
                    S.group("pe", fns, [PBF[b3], CONST], [TB_[1]])
                    cp(xnT[b][:], T[0][:, :].rearrange("p (k n) -> p k n", k=8), [TB_[0]], [XNT[b]])
                    act(pT[b][:], T[1][:, 0:256].rearrange("p (k n) -> p k n", k=2), AF.Copy, [TB_[1]], [PTF[b]])

                def f_tails(ta, tc):
                    if ta is not None:
                        act(junkf[:], x1t[ta][:, :], AF.Square, [X1[ta]], [JF, S1[ta]], accum_out=ss1[:, ta:ta + 1])
                        act(rs1[:, ta:ta + 1], ss1[:, ta:ta + 1], AF.Sqrt, [], [S1[ta]], scale=1.0 / D, bias=EPS)
                    if tc is not None:
                        act(junkf[:], x1t[tc][:, :], AF.Square, [X1[tc]], [JF, S2[tc]], accum_out=ss2[:, tc:tc + 1])
                        act(rs2[:, tc:tc + 1], ss2[:, tc:tc + 1], AF.Sqrt, [], [S2[tc]], scale=1.0 / D, bias=EPS)
                    if ta is not None:
                        recip(rs1[:, ta:ta + 1], rs1[:, ta:ta + 1], [], [S1[ta]])
                        f_B1(ta)
                    if tc is not None:
                        b = tc % 2
                        recip(rs2[:, tc:tc + 1], rs2[:, tc:tc + 1], [], [S2[tc]])
                        stt(ob[b][:], x1t[tc][:, :], rs2[:, tc:tc + 1], gfin[:], ALU.mult, ALU.mult, [X1[tc], S2[tc], GF], [OB[b]])

                def f_C1(t):
                    b = t % 2
                    for half in range(2):
                        hs = slice(half * 512, (half + 1) * 512)
                        pg_, pp_ = (2, 3) if half == 0 else (4, 5)
                        acc_group(P[pg_][:, :], [(xnT[b][:, kc, :], wpg[:, kc, hs]) for kc in range(8)], [XNT[b], WPG], [PB_[pg_]])
                        acc_group(P[pp_][:, :], [(pT[b][:, kc, :], wpp[:, kc, hs]) for kc in range(2)], [PTF[b], WPP], [PB_[pp_]])
                        act(tg[b][:, hs], P[pg_][:, :], AF.Tanh, [PB_[pg_]], [TG[b]], scale=0.5)
                        stt(tg[b][:, hs], tg[b][:, hs], 1.0, P[pp_][:, :], ALU.add, ALU.mult, [PB_[pp_]], [TG[b]])
                        stt(x1t[t][:, hs], tg[b][:, hs], 0.5, x1t[t][:, hs], ALU.mult, ALU.add, [TG[b]], [X1[t]])

                def f_out(t):
                    b = t % 2
                    outs.append(dma("sp", y_d[t * 128:(t + 1) * 128, :], ob[b][:], "o", [OB[b]], []))

                def f_xin(t):
                    b = t % 2
                    dma("sp", xin[b][:], x_d[t * 128:(t + 1) * 128, :], "x", [], [XIN[b]])

                def f_pin(t):
                    b = t % 2
                    dma("sp", pin[b][:], p_d[t * 128:(t + 1) * 128, :], "x", [], [PIN[b]])

                f_xin(0)
                for t in range(NT + 5):
                    if t + 1 < NT:
                        f_xin(t + 1)
                    if t < NT:
                        f_pin(t)
                    if 0 <= t - 5 < NT:
                        f_out(t - 5)
                    if t < NT:
                        f_A1(t)
                    if 0 <= t - 2 < NT:
                        f_B2(t - 2)
                    f_tails(t if t < NT else None, (t - 4) if 0 <= t - 4 < NT else None)
                    if 0 <= t - 3 < NT:
                        f_C1(t - 3)
                S.muted = False
                S.op("sp", lambda e: e.nop(), extra=[t_ for t_ in outs + dbg_tickets if t_ is not None])

        if PRUNE:
            S.finalize()
        with nc.Block() as block:
            @block.sync
            def _(e):
                S.replay("sp", e)

            @block.scalar
            def _(e):
                S.replay("act", e)

            @block.vector
            def _(e):
                S.replay("dve", e)

            @block.gpsimd
            def _(e):
                S.replay("pool", e)

            @block.tensor
            def _(e):
                S.replay("pe", e)
    return nc


_NC_CACHE = {}


def _get_nc(debug=False):
    if debug not in _NC_CACHE:
        _NC_CACHE[debug] = build_nc(debug)
    return _NC_CACHE[debug]


def make_in_maps(inputs, n=8):
    f = lambda a: np.ascontiguousarray(np.asarray(a, dtype=np.float32))
    x = f(inputs["x"])
    p = f(inputs["p"])[0]
    shared = {
        "attn_norm_g": f(inputs["attn_norm_g"]).reshape(1, D),
        "w_in": f(inputs["w_in"])[0],
        "conv_w": f(inputs["conv_w"])[0],
        "conv_b": f(inputs["conv_b"]).reshape(1, 1024),
        "a_bias_i": f(inputs["a_bias_i"]).reshape(1, 4),
        "a_bias_f": f(inputs["a_bias_f"]).reshape(1, 4),
        "a_head_norm_g": f(inputs["a_head_norm_g"]).reshape(1, 1024),
        "b_bias_f": f(inputs["b_bias_f"]).reshape(1, 16),
        "w_branch_a": f(inputs["w_branch_a"])[0],
        "w_branch_b": f(inputs["w_branch_b"])[0],
        "w_out": f(inputs["w_out"])[0],
        "ple_norm_g": f(inputs["ple_norm_g"]).reshape(1, D),
        "w_ple_gate": f(inputs["w_ple_gate"])[0],
        "w_ple_proj": f(inputs["w_ple_proj"])[0],
        "final_norm_g": f(inputs["final_norm_g"]).reshape(1, D),
    }
    maps = []
    for b in range(n):
        m = dict(shared)
        m["x"] = np.ascontiguousarray(x[b])
        m["p"] = np.ascontiguousarray(p[b])
        maps.append(m)
    return maps


def kernel(**inputs):
    nc = _get_nc(False)
    in_maps = make_in_maps(inputs, 8)
    res = run_bass_kernel_spmd(nc, in_maps, core_ids=list(range(8)))
    out = np.stack([np.asarray(r["y"], dtype=np.float32) for r in res.results], axis=0)
    return out
```
